# Optimizing a Trainium2 kernel written in Bass

```python
import math
import jax, jax.numpy as jnp
from jax import lax
import numpy as np

D_MODEL = 1024
BATCH = 16
SEQ = 2048
DEPTH = 4

GRID_W = 64
CTX_LEN = 256
N_MIXERS = 3
N_MOD = 9
D_FF = 2816
MACARON_W = 0.5
NORM_EPS = 1e-6
NEG_INF = -1e9

DA_HEADS = 8
DA_HEAD_DIM = 64
DA_Q_BLOCK = 128
ROPE_BASE = 10000.0
ROPE_PAIRS = DA_HEAD_DIM // 4

NA_HEADS = 16
NA_HEAD_DIM = D_MODEL // NA_HEADS
NA_ROWS = 8
NA_COLS = 16

RW_HEAD = 64
RW_HEADS = D_MODEL // RW_HEAD
RW_DECAY_LORA = 64
RW_AAA_LORA = 64
RW_GATE_LORA = 160
RW_LN_EPS = 64e-5

kernel_name = 'hybrid_diffattn_natten_rwkv7_dit'


def _rmsnorm(h, g):
    hf = h.astype(jnp.float32)
    hf = hf * lax.rsqrt(jnp.mean(hf * hf, axis=-1, keepdims=True) + NORM_EPS)
    return hf.astype(h.dtype) * g


def _modulate(h, shift, scale):
    return h * (1 + scale) + shift


def _ffn_sublayer(h, mod, g_pre, g_post, w_in, w_out):
    shift, scale, gate = mod
    u = _modulate(_rmsnorm(h, g_pre), shift, scale)
    a, b = jnp.split(u @ w_in, 2, axis=-1)
    y = (jax.nn.silu(a) * b) @ w_out
    return h + MACARON_W * gate * _rmsnorm(y, g_post)


def _axial_rope_tables(seq_len):
    t = jnp.arange(seq_len)
    pos = jnp.stack([t // GRID_W, t % GRID_W], axis=-1).astype(jnp.float32)
    freq = ROPE_BASE ** (-jnp.arange(ROPE_PAIRS, dtype=jnp.float32) / ROPE_PAIRS)
    ang = pos[:, :, None] * freq
    return jnp.cos(ang), jnp.sin(ang)


def _axial_rope(x, cos, sin):
    d = x.shape[-1]
    xr = x.reshape(x.shape[:-1] + (2, 2, d // 4))
    x1, x2 = xr[..., 0, :], xr[..., 1, :]
    bshape = (cos.shape[0],) + (1,) * (x.ndim - 3) + (2, d // 4)
    c = cos.reshape(bshape).astype(x.dtype)
    s = sin.reshape(bshape).astype(x.dtype)
    return jnp.stack([x1 * c - x2 * s, x2 * c + x1 * s], axis=-2).reshape(x.shape)


def _attend(q, k, v):
    s = jnp.einsum('bthd,bkhd->bhtk', q, k).astype(jnp.float32)
    p = jax.nn.softmax(s, axis=-1).astype(v.dtype)
    return jnp.einsum('bhtk,bkhd->bthd', p, v)


def _diff_attend(q, k, v, lam):
    s = jnp.einsum('bthjd,bkhjd->bhjtk', q, k).astype(jnp.float32)
    p = jax.nn.softmax(s, axis=-1)
    p = p[:, :, 0] - lam.astype(jnp.float32) * p[:, :, 1]
    return jnp.einsum('bhtk,bkhe->bthe', p.astype(v.dtype), v)


def _diff_attention(uc, ux, w_in, w_out, lam_vec, subln_g, lam_init, cos, sin, need_ctx):
    B, S, D = ux.shape
    H, d = DA_HEADS, DA_HEAD_DIM

    def proj(u):
        L = u.shape[1]
        q, k, v = jnp.split(u @ w_in, 3, axis=-1)
        return (q.reshape(B, L, H, 2, d) * d ** -0.5, k.reshape(B, L, H, 2, d), v.reshape(B, L, H, 2 * d))

    qc, kc, vc = proj(uc)
    qx, kx, vx = proj(ux)
    qx = _axial_rope(qx, cos, sin)
    kx = _axial_rope(kx, cos, sin)
    lam = (jnp.exp(jnp.sum(lam_vec[0] * lam_vec[1])) - jnp.exp(jnp.sum(lam_vec[2] * lam_vec[3]))
           + lam_init)
    k_all = jnp.concatenate([kc, kx], axis=1)
    v_all = jnp.concatenate([vc, vx], axis=1)
    nb = S // DA_Q_BLOCK
    qb = jnp.moveaxis(qx.reshape(B, nb, DA_Q_BLOCK, H, 2, d), 1, 0)
    ox = lax.map(lambda q: _diff_attend(q, k_all, v_all, lam), qb)
    ox = jnp.moveaxis(ox, 0, 1).reshape(B, S, H, 2 * d)

    def out(o):
        o = _rmsnorm(o, subln_g) * (1 - lam_init)
        return o.reshape(o.shape[0], o.shape[1], H * 2 * d) @ w_out

    yx = out(ox)
    yc = out(_diff_attend(qc, kc, vc, lam)) if need_ctx else None
    return yc, yx


def _neighbourhood_attention(uc, ux, w_in, w_out, rpb, need_ctx):
    B, S, D = ux.shape
    C = uc.shape[1]
    H, d = NA_HEADS, NA_HEAD_DIM
    rows = S // GRID_W
    kr = min(NA_ROWS, rows)

    def proj(u):
        L = u.shape[1]
        q, k, v = jnp.split(u @ w_in, 3, axis=-1)
        return q.reshape(B, L, H, d) * d ** -0.5, k.reshape(B, L, H, d), v.reshape(B, L, H, d)

    qc, kc, vc = proj(uc)
    qx, kx, vx = proj(ux)
    q_grid = qx.reshape(B, rows, GRID_W, H, d)
    k_grid = kx.reshape(B, rows, GRID_W, H, d)
    v_grid = vx.reshape(B, rows, GRID_W, H, d)
    col = jnp.arange(GRID_W)
    col_start = jnp.clip(col - NA_COLS // 2, 0, GRID_W - NA_COLS)
    col_in = (col[None, :] >= col_start[:, None]) & (col[None, :] < col_start[:, None] + NA_COLS)
    col_off = jnp.clip(col[None, :] - col[:, None] + NA_COLS - 1, 0, 2 * NA_COLS - 2)
    col_bias = jnp.where(col_in, rpb[:, :, col_off].astype(jnp.float32), NEG_INF)

    def row_block(r):
        rs = jnp.clip(r - kr // 2, 0, rows - kr)
        q = lax.dynamic_index_in_dim(q_grid, r, axis=1, keepdims=False)
        kb = lax.dynamic_slice_in_dim(k_grid, rs, kr, axis=1)
        vb = lax.dynamic_slice_in_dim(v_grid, rs, kr, axis=1)
        bias = jnp.swapaxes(col_bias[:, rs + jnp.arange(kr) - r + NA_ROWS - 1], 1, 2)
        s_lat = jnp.einsum('bqhd,brkhd->bhqrk', q, kb).astype(jnp.float32) + bias
        s_ctx = jnp.einsum('bqhd,bchd->bhqc', q, kc).astype(jnp.float32)
        s = jnp.concatenate([s_ctx, s_lat.reshape(B, H, GRID_W, kr * GRID_W)], axis=-1)
        p = jax.nn.softmax(s, axis=-1).astype(vx.dtype)
        p_ctx = p[..., :C]
        p_lat = p[..., C:].reshape(B, H, GRID_W, kr, GRID_W)
        return (jnp.einsum('bhqc,bchd->bqhd', p_ctx, vc)
                + jnp.einsum('bhqrk,brkhd->bqhd', p_lat, vb))

    ox = lax.map(row_block, jnp.arange(rows))
    yx = jnp.moveaxis(ox, 0, 1).reshape(B, S, D) @ w_out
    yc = _attend(qc, kc, vc).reshape(B, C, D) @ w_out if need_ctx else None
    return yc, yx


def _rwkv7_prep(u, mu, w_in, w0, w1, w2, a0, a1, a2, g1, g2, k_k, k_a):
    B, L, D = u.shape
    H, N = RW_HEADS, RW_HEAD
    f32 = jnp.float32
    zero = jnp.zeros_like(u[:, :1])
    xx = 0.5 * (jnp.concatenate([zero, u[:, :-1]], axis=1) + jnp.concatenate([u[:, 1:], zero], axis=1)) - u
    xr, xw, xk, xv, xa, xg = [u + xx * mu[j] for j in range(6)]
    w_r, w_k, w_v = jnp.split(w_in, 3, axis=1)
    r = xr @ w_r
    k = xk @ w_k
    v = xv @ w_v
    lora_w = jnp.einsum('zblr,zrd->zbld', jnp.tanh(jnp.einsum('bld,zdr->zblr', xw, w1)), w2)
    w_log = -jax.nn.softplus(-(w0[:, None, None, :] + lora_w).astype(f32)) - 0.5
    decay = jnp.exp(-jnp.exp(w_log))
    a = jax.nn.sigmoid(a0[:, None, None, :]
                       + jnp.einsum('zblr,zrd->zbld', jnp.einsum('bld,zdr->zblr', xa, a1), a2))
    g = jax.nn.sigmoid(xg @ g1) @ g2
    kk = (k * k_k).reshape(B, L, H, N).astype(f32)
    kk = kk / jnp.maximum(jnp.linalg.norm(kk, axis=-1, keepdims=True), 1e-12)
    kd = k[None] * (1 + (a - 1) * k_a)

    def heads(t):
        return t.reshape(t.shape[:-1] + (H, N)).astype(f32)

    def two(t):
        return jnp.broadcast_to(t, (2,) + t.shape)

    def orient(t):
        t = jnp.stack([t[0], jnp.flip(t[1], axis=1)])
        return jnp.moveaxis(t, 2, 0)

    scan_in = (orient(two(heads(r))), orient(heads(decay)), orient(heads(kd)),
               orient(two(heads(v))), orient(two(-kk)), orient(kk[None] * heads(a)))
    return scan_in, (r, kd, v, g)


def _wkv7_scan(state0, r, w, k, v, a, b):
    def step(S, inp):
        rt, wt, kt, vt, at, bt = inp
        sa = jnp.einsum('zbhvk,zbhk->zbhv', S, at)
        S = S * wt[..., None, :] + sa[..., :, None] * bt[..., None, :] + vt[..., :, None] * kt[..., None, :]
        return S, jnp.einsum('zbhvk,zbhk->zbhv', S, rt)
    return lax.scan(step, state0, (r, w, k, v, a, b))


def _rwkv7_readout(y, r, kd, v, g, r_k, ln_g, ln_b, w_out):
    B, L, D = r.shape
    H, N = RW_HEADS, RW_HEAD
    y = jnp.moveaxis(y, 0, 2)
    y = y[0] + jnp.flip(y[1], axis=1)
    mean = jnp.mean(y, axis=-1, keepdims=True)
    var = jnp.mean(jnp.square(y - mean), axis=-1, keepdims=True)
    y = ((y - mean) * lax.rsqrt(var + RW_LN_EPS)).reshape(B, L, D).astype(r.dtype) * ln_g + ln_b
    rh = r.reshape(B, L, H, N)
    kh = (kd[0] + kd[1]).reshape(B, L, H, N)
    bonus = jnp.sum(rh * kh * r_k, axis=-1, keepdims=True) * v.reshape(B, L, H, N)
    return ((y + bonus.reshape(B, L, D)) * g) @ w_out


def _rwkv7_mixer(uc, ux, mu, w_in, w_out, w0, w1, w2, a0, a1, a2, g1, g2, k_k, k_a, r_k,
                 ln_g, ln_b, need_ctx):
    B = ux.shape[0]
    in_c, aux_c = _rwkv7_prep(uc, mu, w_in, w0, w1, w2, a0, a1, a2, g1, g2, k_k, k_a)
    in_x, aux_x = _rwkv7_prep(ux, mu, w_in, w0, w1, w2, a0, a1, a2, g1, g2, k_k, k_a)
    s0 = jnp.zeros((2, B, RW_HEADS, RW_HEAD, RW_HEAD), jnp.float32)
    s_ctx, y_c = _wkv7_scan(s0, *in_c)
    _, y_x = _wkv7_scan(s_ctx, *in_x)
    yx = _rwkv7_readout(y_x, *aux_x, r_k, ln_g, ln_b, w_out)
    yc = _rwkv7_readout(y_c, *aux_c, r_k, ln_g, ln_b, w_out) if need_ctx else None
    return yc, yx


def setup_inputs(seed: int = 0) -> dict:
    key = jax.random.key(seed)
    ks = iter(jax.random.split(key, 48))
    f32 = jnp.float32
    D, F = D_MODEL, D_FF
    n_a, n_b, n_c = [len(range(m, DEPTH, N_MIXERS)) for m in range(N_MIXERS)]

    def nrm(shape, scale):
        return jax.random.normal(next(ks), shape, f32) * scale

    def gain(shape):
        return 1.0 + nrm(shape, 0.05)

    return {
        'x': nrm((BATCH, SEQ, D), 1.0),
        'c': nrm((BATCH, D), 1.0),
        'ctx': nrm((BATCH, CTX_LEN, D), 1.0),
        'c_ctx': nrm((D,), 1.0),
        'ada_w': nrm((DEPTH, D, N_MOD * D), 0.5 * D ** -0.5),
        'ada_b': nrm((DEPTH, N_MOD * D), 0.02),
        'norm_g': gain((DEPTH, 6, D)),
        'ffn_w_in': nrm((DEPTH, 2, D, 2 * F), D ** -0.5),
        'ffn_w_out': nrm((DEPTH, 2, F, D), F ** -0.5),
        'da_w_in': nrm((n_a, D, 3 * D), D ** -0.5),
        'da_w_out': nrm((n_a, D, D), D ** -0.5),
        'da_lambda': nrm((n_a, 4, DA_HEAD_DIM), 0.1),
        'da_subln_g': gain((n_a, 2 * DA_HEAD_DIM)),
        'na_w_in': nrm((n_b, D, 3 * D), D ** -0.5),
        'na_w_out': nrm((n_b, D, D), D ** -0.5),
        'na_rpb': nrm((n_b, NA_HEADS, 2 * NA_ROWS - 1, 2 * NA_COLS - 1), 0.2),
        'rw_mu': jax.random.uniform(next(ks), (n_c, 6, D), f32),
        'rw_w_in': nrm((n_c, D, 3 * D), D ** -0.5),
        'rw_w_out': nrm((n_c, D, D), D ** -0.5),
        'rw_w0': jax.random.uniform(next(ks), (n_c, 2, D), f32, -6.0, -1.0),
        'rw_w1': nrm((n_c, 2, D, RW_DECAY_LORA), D ** -0.5),
        'rw_w2': nrm((n_c, 2, RW_DECAY_LORA, D), 0.1 * RW_DECAY_LORA ** -0.5),
        'rw_a0': nrm((n_c, 2, D), 0.1),
        'rw_a1': nrm((n_c, 2, D, RW_AAA_LORA), D ** -0.5),
        'rw_a2': nrm((n_c, 2, RW_AAA_LORA, D), RW_AAA_LORA ** -0.5),
        'rw_g1': nrm((n_c, D, RW_GATE_LORA), D ** -0.5),
        'rw_g2': nrm((n_c, RW_GATE_LORA, D), RW_GATE_LORA ** -0.5),
        'rw_k_k': 0.85 + nrm((n_c, D), 0.05),
        'rw_k_a': gain((n_c, D)),
        'rw_r_k': nrm((n_c, RW_HEADS, RW_HEAD), 0.1),
        'rw_ln_g': gain((n_c, D)),
        'rw_ln_b': nrm((n_c, D), 0.02),
    }


def reference(x, c, ctx, c_ctx, ada_w, ada_b, norm_g, ffn_w_in, ffn_w_out,
              da_w_in, da_w_out, da_lambda, da_subln_g,
              na_w_in, na_w_out, na_rpb,
              rw_mu, rw_w_in, rw_w_out, rw_w0, rw_w1, rw_w2, rw_a0, rw_a1, rw_a2,
              rw_g1, rw_g2, rw_k_k, rw_k_a, rw_r_k, rw_ln_g, rw_ln_b):
    B, S, D = x.shape
    cos, sin = _axial_rope_tables(S)
    hx, hc = x, ctx
    silu_c, silu_cc = jax.nn.silu(c), jax.nn.silu(c_ctx)
    for i in range(DEPTH):
        last = i == DEPTH - 1
        kind, slot = i % N_MIXERS, i // N_MIXERS
        g = norm_g[i]
        mx = (silu_c @ ada_w[i] + ada_b[i]).reshape(B, N_MOD, 1, D)
        mc = (silu_cc @ ada_w[i] + ada_b[i]).reshape(N_MOD, D)

        hx = _ffn_sublayer(hx, (mx[:, 0], mx[:, 1], mx[:, 2]), g[0], g[1], ffn_w_in[i, 0], ffn_w_out[i, 0])
        hc = _ffn_sublayer(hc, (mc[0], mc[1], mc[2]), g[0], g[1], ffn_w_in[i, 0], ffn_w_out[i, 0])

        ux = _modulate(_rmsnorm(hx, g[2]), mx[:, 3], mx[:, 4])
        uc = _modulate(_rmsnorm(hc, g[2]), mc[3], mc[4])
        if kind == 0:
            lam_init = 0.8 - 0.6 * math.exp(-0.3 * i)
            yc, yx = _diff_attention(uc, ux, da_w_in[slot], da_w_out[slot], da_lambda[slot],
                                     da_subln_g[slot], lam_init, cos, sin, not last)
        elif kind == 1:
            yc, yx = _neighbourhood_attention(uc, ux, na_w_in[slot], na_w_out[slot], na_rpb[slot], not last)
        else:
            yc, yx = _rwkv7_mixer(uc, ux, rw_mu[slot], rw_w_in[slot], rw_w_out[slot],
                                  rw_w0[slot], rw_w1[slot], rw_w2[slot],
                                  rw_a0[slot], rw_a1[slot], rw_a2[slot],
                                  rw_g1[slot], rw_g2[slot], rw_k_k[slot], rw_k_a[slot],
                                  rw_r_k[slot], rw_ln_g[slot], rw_ln_b[slot], not last)
        hx = hx + mx[:, 5] * _rmsnorm(yx, g[3])

        hx = _ffn_sublayer(hx, (mx[:, 6], mx[:, 7], mx[:, 8]), g[4], g[5], ffn_w_in[i, 1], ffn_w_out[i, 1])
        if not last:
            hc = hc + mc[5] * _rmsnorm(yc, g[3])
            hc = _ffn_sublayer(hc, (mc[6], mc[7], mc[8]), g[4], g[5], ffn_w_in[i, 1], ffn_w_out[i, 1])
    return hx
```

```python
import numpy as np
from contextlib import ExitStack
import concourse.bass as bass
import concourse.mybir as mybir

F32 = mybir.dt.float32
BF16 = mybir.dt.bfloat16
AF = mybir.ActivationFunctionType
ALU = mybir.AluOpType
AX = mybir.AxisListType


class T:
    __slots__ = ("h", "name", "lastw", "readers")

    def __init__(self, h, name):
        self.h = h
        self.name = name
        self.lastw = {}
        self.readers = {}

    def __getitem__(self, idx):
        return self.h[idx]


class K:
    def __init__(self, nc, n_dma_sems=20):
        self.nc = nc
        self.es = ExitStack()
        self.engs = {}
        for nm, h in (("pe", nc.tensor), ("act", nc.scalar), ("dve", nc.vector),
                      ("pool", nc.gpsimd), ("sp", nc.sync)):
            sem = self.es.enter_context(nc.semaphore("s_" + nm))
            self.engs[nm] = dict(h=h, sem=sem, cnt=0, waited={})
        self.dma_pool = {}
        for q in ("sp", "pool", "act"):
            sems = [self.es.enter_context(nc.semaphore(f"d_{q}{i}")) for i in range(n_dma_sems)]
            self.dma_pool[q] = dict(sems=sems, uses=[0] * n_dma_sems, nxt=0)
        self.semkey = {}
        self.phase_stack = None
        self.n_inst = 0

    def sb(self, name, shape, dt, stack=None):
        st = stack if stack is not None else self.es
        self.uid = getattr(self, "uid", 0) + 1
        name = f"{name}_{self.uid}"
        h = st.enter_context(self.nc.sbuf_tensor(name, list(shape), dt))
        return T(h, name)

    def ps(self, name, shape, dt, stack=None):
        st = stack if stack is not None else self.es
        h = st.enter_context(self.nc.psum_tensor(name, list(shape), dt))
        return T(h, name)

    def dram(self, name, shape, dt, kind="Internal"):
        h = self.nc.dram_tensor(name, list(shape), dt, kind=kind)
        return T(h, name)

    def _wait(self, eng, dep):
        sem, val, key = dep
        assert val is not None, "dependency on a non-incrementing instruction with no later incrementing one"
        e = self.engs[eng]
        if e["waited"].get(key, 0) >= val:
            return
        e["h"].wait_ge(sem, val)
        e["waited"][key] = val

    def _deps(self, reads, writes):
        deps = []
        for (t, s) in reads:
            for kk in ((s, None) if s is not None else tuple(t.lastw.keys())):
                d = t.lastw.get(kk)
                if d is not None:
                    deps.append(d)
        for (t, s) in writes:
            keys = (s, None) if s is not None else tuple(set(t.lastw.keys()) | set(t.readers.keys()))
            for kk in keys:
                d = t.lastw.get(kk)
                if d is not None:
                    deps.append(d)
                deps.extend(t.readers.get(kk, {}).values())
        return deps

    def _record(self, reads, writes, dep):
        for (t, s) in reads:
            r = t.readers.setdefault(s, {})
            old = r.get(dep[2])
            if old is None or dep[1] is None or (old[1] is not None and old[1] < dep[1]):
                r[dep[2]] = dep
        for (t, s) in writes:
            if s is None:
                t.lastw = {None: dep}
                t.readers = {}
            else:
                t.lastw[s] = dep
                t.readers[s] = {}

    @staticmethod
    def _norm(lst):
        out = []
        for x in lst:
            if isinstance(x, tuple):
                out.append(x)
            else:
                out.append((x, None))
        return out

    def op(self, eng, fn, reads=(), writes=(), inc=True):
        reads = self._norm(reads)
        writes = self._norm(writes)
        e = self.engs[eng]
        for d in self._deps(reads, writes):
            if eng == "pe" and d[2] == "pe":
                continue
            self._wait(eng, d)
        ins = fn(e["h"])
        import os
        if not os.environ.get("FW_LAZYINC"):
            inc = True
        if inc:
            e["cnt"] += 1
            ins.then_inc(e["sem"], 1)
            dep = [e["sem"], e["cnt"], eng]
            for p in e.setdefault("pending", []):
                p[1] = e["cnt"]
            e["pending"] = []
        else:
            dep = [e["sem"], None, eng]
            e.setdefault("pending", []).append(dep)
        self._record(reads, writes, dep)
        self.n_inst += 1
        return ins

    def dma(self, q, out_ap, in_ap, reads=(), writes=(), **kw):
        reads = self._norm(reads)
        writes = self._norm(writes)
        e = self.engs[q]
        p = self.dma_pool[q]
        i = p["nxt"]
        p["nxt"] = (i + 1) % len(p["sems"])
        sem = p["sems"][i]
        key = f"d_{q}{i}"
        if p["uses"][i] > 0:
            self._wait(q, (sem, 16 * p["uses"][i], key))
        for d in self._deps(reads, writes):
            self._wait(q, d)
        p["uses"][i] += 1
        ins = e["h"].dma_start(out=out_ap, in_=in_ap, **kw)
        ins.then_inc(sem, 16)
        dep = [sem, 16 * p["uses"][i], key]
        self._record(reads, writes, dep)
        self.n_inst += 1
        return dep

    def barrier(self):
        deps = []
        for nm, e in self.engs.items():
            if e["cnt"] > 0:
                deps.append((e["sem"], e["cnt"], nm))
        for q, p in self.dma_pool.items():
            for i, s in enumerate(p["sems"]):
                if p["uses"][i] > 0:
                    deps.append((s, 16 * p["uses"][i], f"d_{q}{i}"))
        for nm in self.engs:
            for d in deps:
                if d[2] == nm:
                    continue
                self._wait(nm, d)

    def finish(self):
        self.barrier()
        self.es.close()


def _mm(k, out, lhsT, rhs, start, stop, reads, writes, inc=None):
    return k.op("pe", lambda e: e.matmul(out, lhsT, rhs, start=start, stop=stop), reads, writes,
                inc=(stop if inc is None else inc))


def _tr(k, out, in_, ident, reads, writes):
    return k.op("pe", lambda e: e.transpose(out, in_, ident), reads, writes)


def _act(k, out, in_, func, reads, writes, bias=0.0, scale=1.0):
    return k.op("act", lambda e: e.activation(out, in_, func, bias=bias, scale=scale), reads, writes)


def _tt(k, eng, out, in0, in1, op, reads, writes):
    return k.op(eng, lambda e: e.tensor_tensor(out, in0, in1, op=op), reads, writes)


def _ts(k, eng, out, in0, s1, s2, op0, op1, reads, writes):
    if op1 is None:
        return k.op(eng, lambda e: e.tensor_scalar(out, in0, s1, None, op0=op0), reads, writes)
    return k.op(eng, lambda e: e.tensor_scalar(out, in0, s1, s2, op0=op0, op1=op1), reads, writes)


def _stt(k, eng, out, in0, scalar, in1, op0, op1, reads, writes):
    return k.op(eng, lambda e: e.scalar_tensor_tensor(out=out, in0=in0, scalar=scalar, in1=in1, op0=op0, op1=op1), reads, writes)


def _cp(k, eng, out, in_, reads, writes):
    if eng == "act":
        return k.op("act", lambda e: e.copy(out, in_), reads, writes)
    return k.op(eng, lambda e: e.tensor_copy(out, in_), reads, writes)


def _rcp(k, out, in_, reads, writes):
    return k.op("dve", lambda e: e.reciprocal(out, in_), reads, writes)


def _memset(k, eng, ap, val, writes):
    return k.op(eng, lambda e: e.memset(ap, val), (), writes)


K.mm = _mm; K.tr = _tr; K.act = _act; K.tt = _tt; K.ts = _ts; K.stt = _stt; K.cp = _cp; K.rcp = _rcp; K.memset = _memset

from concourse.bass_utils import run_bass_kernel_spmd

D = 1024
C = 256
S = 2048
TT = C + S
F = 2816
NBC = 2
EPS = 1e-6
NT = 256

V_NORMG = 0
V_ADAB = 24
V_RW = 60
(V_MU, V_W0, V_A0, V_KK, V_KA, V_LNG, V_LNB, V_RK) = (60, 66, 68, 70, 71, 72, 73, 74)
NV = 75


class Ctx:
    pass


class Scope:
    def __init__(self, k):
        self.k = k
        self.st = ExitStack()

    def __enter__(self):
        self.st.__enter__()
        return self.st

    def __exit__(self, *a):
        if a[0] is None:
            self.k.barrier()
        return self.st.__exit__(*a)


def hslots(g, b, t0, nt):
    return [(g.hT, (b, i)) for i in range(t0 // NT, (t0 + nt + NT - 1) // NT)]


def build_program(stop_after=None, dbg=False):
    nc = bass.Bass("TRN2", target_bir_lowering=False)
    k = K(nc)
    g = Ctx()
    g.k = k
    def ein(name, shape):
        return k.dram(name, shape, F32, kind="ExternalInput")
    g.x = ein("x", [NBC, S, D]); g.ctx = ein("ctx", [NBC, C, D]); g.cvec = ein("cvec", [3, D])
    g.vecs = ein("vecs", [NV, D]); g.identd = ein("ident", [128, 128])
    g.ada_w = ein("ada_w", [4, D, 9 * D])
    g.ffn_w_in = ein("ffn_w_in", [4, 2, D, 2 * F]); g.ffn_w_out = ein("ffn_w_out", [4, 2, F, D])
    g.da_w_in = ein("da_w_in", [2, D, 3 * D]); g.da_w_sw = ein("da_w_sw", [2, D, 2 * D]); g.da_w_out = ein("da_w_out", [2, D, D])
    g.da_lambda = ein("da_lambda", [2, 4, 64]); g.da_subln = ein("da_subln_g", [2, 128])
    g.rope = ein("rope", [2, 128, S])
    g.na_w_in = ein("na_w_in", [1, D, 3 * D]); g.na_w_out = ein("na_w_out", [1, D, D]); g.na_bias = ein("na_bias", [64, 16, NA_BW])
    g.rw_w_in = ein("rw_w_in", [1, D, 3 * D]); g.rw_w_out = ein("rw_w_out", [1, D, D])
    g.rw_w1 = ein("rw_w1", [1, 2, D, 64]); g.rw_w2 = ein("rw_w2", [1, 2, 64, D])
    g.rw_a1 = ein("rw_a1", [1, 2, D, 64]); g.rw_a2 = ein("rw_a2", [1, 2, 64, D])
    g.rw_g1 = ein("rw_g1", [1, D, 160]); g.rw_g2 = ein("rw_g2", [1, 160, D])
    g.rwconst = ein("rwconst", [5, 128, 128])
    g.out = k.dram("out", [NBC, S, D], F32, kind="ExternalOutput")
    g.hT = k.dram("hT", [NBC, D, TT], F32)
    if dbg:
        g.dbg = k.dram("dbg", [NBC, D, TT], F32, kind="ExternalOutput")

    g.ident = k.sb("identf", [128, 128], F32)
    g.identb = k.sb("identb", [128, 128], BF16)
    g.ones = k.sb("onesb", [128, 128], BF16)
    g.vp = k.sb("vp", [128, NV, 8], F32)
    g.mod = k.sb("mod", [128, 36, 8, 3], F32)
    g.mA = k.sb("mA", [128, 8, 3], F32); g.mB = k.sb("mB", [128, 8, 3], F32); g.mC = k.sb("mC", [128, 8, 3], F32)
    g.PS = [k.ps(f"ps{i}", [128, 512], F32) for i in range(8)]
    k.dma("sp", g.ident[:], g.identd[:], reads=[g.identd], writes=[g.ident])
    k.cp("dve", g.identb[:], g.ident[:], [g.ident], [g.identb])
    k.memset("dve", g.ones[:], 1.0, [g.ones])

    prologue(g)
    done = False
    for l in range(4):
        for ph in ("ffn0", "mix", "ffn1"):
            if done:
                break
            k.barrier()
            if ph == "ffn0":
                ffn_phase(g, l, 0)
            elif ph == "ffn1":
                ffn_phase(g, l, 1)
            else:
                if l % 3 == 0:
                    da_phase(g, l)
                elif l % 3 == 1:
                    na_phase(g, l)
                else:
                    rw_phase(g, l)
            if stop_after == (l, ph):
                done = True
    k.barrier()
    epilogue(g, dbg)
    k.finish()
    return nc


def prologue(g):
    k = g.k
    PS = g.PS
    with Scope(k) as st:
        vrow = k.sb("vrow", [NV, D], F32, st)
        crow = k.sb("crow", [3, D], F32, st)
        scT = k.sb("scT", [128, 8, 3], F32, st)
        k.dma("sp", vrow[:], g.vecs[:], reads=[g.vecs], writes=[vrow])
        k.dma("sp", crow[:], g.cvec[:], reads=[g.cvec], writes=[crow])
        k.act(crow[:], crow[:], AF.Silu, [crow], [crow])
        for dc in range(8):
            p = PS[dc % 2]
            k.tr(p[:, :NV], vrow[:, dc * 128:(dc + 1) * 128], g.ident[:NV, :NV], [vrow, g.ident], [p])
            k.cp("dve", g.vp[:, :, dc], p[:, :NV], [p], [g.vp])
            p2 = PS[2 + dc % 2]
            k.tr(p2[:, :3], crow[:, dc * 128:(dc + 1) * 128], g.ident[:3, :3], [crow, g.ident], [p2])
            k.cp("dve", scT[:, dc, :], p2[:, :3], [p2], [scT])
        aws = [k.sb(f"aw{i}", [128, 8, D], F32, st) for i in range(2)]
        xin = [k.sb(f"xin{i}", [128, D], F32, st) for i in range(2)]
        xst = [k.sb(f"xst{i}", [128, 8, 512], F32, st) for i in range(2)]

        def xpose_jobs():
            n = 0
            for b in range(NBC):
                groups = [(g.ctx, 0, 0, 256)] + [(g.x, i * 512, C + i * 512, 512) for i in range(4)]
                for gi, (src, s0, t0, w) in enumerate(groups):
                    xs = xst[gi % 2]
                    for sub in range(w // 128):
                        xi = xin[n % 2]
                        k.dma("sp", xi[:], src[b, s0 + sub * 128: s0 + (sub + 1) * 128, :], reads=[src], writes=[xi])
                        for half in range(2):
                            p = PS[(n % 2) * 2 + half]
                            for q in range(4):
                                dc = half * 4 + q
                                k.tr(p[:, q * 128:(q + 1) * 128], xi[:, dc * 128:(dc + 1) * 128], g.ident[:], [xi, g.ident], [p])
                            k.cp("act" if half else "dve", xs[:, half * 4:(half + 1) * 4, sub * 128:(sub + 1) * 128],
                                 p[:].rearrange("p (q t) -> p q t", q=4), [p], [xs])
                        n += 1
                        if sub == w // 128 - 1:
                            k.dma("act", g.hT[b, :, t0:t0 + w].rearrange("(c p) t -> p c t", p=128), xs[:, :, :w],
                                  reads=[xs], writes=hslots(g, b, t0, w))
                        yield

        jobs = xpose_jobs()
        n = 0
        for l in range(4):
            for j in range(9):
                aw = aws[n % 2]
                k.dma("sp" if n % 2 == 0 else "act", aw[:],
                      g.ada_w[l, :, j * D:(j + 1) * D].rearrange("(c p) f -> p c f", p=128),
                      reads=[g.ada_w], writes=[aw])
                next(jobs, None)
                pm = PS[4 + n % 2]
                for dc in range(8):
                    for cc in range(8):
                        k.mm(pm[:, dc * 3:(dc + 1) * 3], aw[:, cc, dc * 128:(dc + 1) * 128], scT[:, cc, :],
                             cc == 0, cc == 7, [aw, scT], [pm])
                k.tt("dve", g.mod[:, l * 9 + j, :, :], pm[:, 0:24].rearrange("p (c o) -> p c o", o=3),
                     g.vp[:, V_ADAB + l * 9 + j, :].unsqueeze(2).to_broadcast([128, 8, 3]), ALU.add,
                     [pm, g.vp], [g.mod])
                n += 1
        for _ in jobs:
            pass


def epilogue(g, dbg):
    k = g.k
    PS = g.PS
    with Scope(k) as st:
        xs2 = [k.sb(f"exs{i}", [128, 8, 512], F32, st) for i in range(2)]
        ot = [k.sb(f"eot{i}", [128, D], F32, st) for i in range(2)]
        n = 0
        for b in range(NBC):
            for gi in range(4):
                t0 = C + gi * 512
                xs = xs2[gi % 2]
                k.dma("sp", xs[:], g.hT[b, :, t0:t0 + 512].rearrange("(c p) t -> p c t", p=128),
                      reads=hslots(g, b, t0, 512), writes=[xs])
                for sub in range(4):
                    o = ot[n % 2]
                    for half in range(2):
                        p = PS[(n % 2) * 2 + half]
                        for q in range(4):
                            dc = half * 4 + q
                            k.tr(p[:, q * 128:(q + 1) * 128], xs[:, dc, sub * 128:(sub + 1) * 128], g.ident[:], [xs, g.ident], [p])
                        k.cp("act" if half else "dve", o[:, half * 512:(half + 1) * 512], p[:], [p], [o])
                    k.dma("act", g.out[b, gi * 512 + sub * 128: gi * 512 + (sub + 1) * 128, :], o[:], reads=[o], writes=[g.out])
                    n += 1
        if dbg and not getattr(g, "skip_dbg_copy", False):
            for b in range(NBC):
                k.dma("sp", g.dbg[b], g.hT[b], reads=[g.hT], writes=[g.dbg])


def set_mod(g, l, j_shift, j_scale, j_gate, s_pre, s_post, gate_mul):
    k = g.k
    def gv(s):
        return g.vp[:, V_NORMG + l * 6 + s, :].unsqueeze(2).to_broadcast([128, 8, 3])
    k.ts("dve", g.mA[:], g.mod[:, l * 9 + j_scale, :, :], 1.0, None, ALU.add, None, [g.mod], [g.mA])
    k.tt("dve", g.mA[:], g.mA[:], gv(s_pre), ALU.mult, [g.mA, g.vp], [g.mA])
    k.cp("dve", g.mB[:], g.mod[:, l * 9 + j_shift, :, :], [g.mod], [g.mB])
    k.stt("dve", g.mC[:], g.mod[:, l * 9 + j_gate, :, :], float(gate_mul), gv(s_post), ALU.mult, ALU.mult, [g.mod, g.vp], [g.mC])


def prenorm(g, xT, nt, col, u, sq, tmp, rstd, ps, uoff=0):
    k = g.k
    k.tt("dve", sq[:, :, :nt], xT[:, :, :nt], xT[:, :, :nt], ALU.mult, [xT], [sq])
    for c in range(8):
        k.mm(ps[:, :nt], g.ones[:], sq[:, c, :nt], c == 0, c == 7, [g.ones, sq], [ps])
    k.act(rstd[:, :nt], ps[:, :nt], AF.Sqrt, [ps], [rstd], bias=EPS, scale=1.0 / D)
    k.rcp(rstd[:, :nt], rstd[:, :nt], [rstd], [rstd])
    for c in range(8):
        k.stt("dve", tmp[:, c, :nt], xT[:, c, :nt], g.mA[:, c, col:col + 1], rstd[:, :nt], ALU.mult, ALU.mult,
              [xT, g.mA, rstd], [(tmp, c)])
        k.op("act", lambda e, o=u[:, c, uoff:uoff + nt], i_=tmp[:, c, :nt], a_=g.mB[:, c, col:col + 1]: e.add(o, i_, a_),
             [(tmp, c), g.mB], [(u, c)])


def postnorm_residual(g, y, sq, nt, col, xT, rstd, ps):
    k = g.k
    k.op("act", lambda e: e.square(sq[:, :, :nt], y[:, :, :nt]), [y], [sq])
    for c in range(8):
        k.mm(ps[:, :nt], g.ones[:], sq[:, c, :nt], c == 0, c == 7, [g.ones, sq], [ps])
    k.act(rstd[:, :nt], ps[:, :nt], AF.Sqrt, [ps], [rstd], bias=EPS, scale=1.0 / D)
    k.rcp(rstd[:, :nt], rstd[:, :nt], [rstd], [rstd])
    for c in range(8):
        k.tt("dve", y[:, c, :nt], y[:, c, :nt], rstd[:, :nt], ALU.mult, [(y, c), rstd], [(y, c)])
        k.stt("dve", xT[:, c, :nt], y[:, c, :nt], g.mC[:, c, col:col + 1], xT[:, c, :nt], ALU.mult, ALU.add,
              [(y, c), g.mC, (xT, c)], [(xT, c)])


def token_tiles(skip_ctx=False):
    tl = []
    for b in range(NBC):
        if not skip_ctx:
            tl.append((b, 0, NT, 2))
        for i in range(S // NT):
            tl.append((b, C + i * NT, NT, b))
    return tl


def load_tile(g, xT, b, t0, nt):
    g.k.dma("sp", xT[:, :, :nt], g.hT[b, :, t0:t0 + nt].rearrange("(c p) t -> p c t", p=128),
            reads=hslots(g, b, t0, nt), writes=[xT])


def store_tile(g, xT, b, t0, nt):
    g.k.dma("sp", g.hT[b, :, t0:t0 + nt].rearrange("(c p) t -> p c t", p=128), xT[:, :, :nt],
            reads=[xT], writes=hslots(g, b, t0, nt))


def ffn_phase(g, l, s):
    k = g.k
    PS = g.PS
    with Scope(k) as st:
        wA = k.sb("wA", [128, 8, 2 * F], BF16, st)
        wB = k.sb("wB", [128, 22, D], BF16, st)
        for c in range(8):
            k.dma("pool", wA[:, c, :], g.ffn_w_in[l, s, c * 128:(c + 1) * 128, :], reads=[g.ffn_w_in], writes=[(wA, c)])
        for j in range(22):
            k.dma("pool", wB[:, j, :], g.ffn_w_out[l, s, j * 128:(j + 1) * 128, :], reads=[g.ffn_w_out], writes=[(wB, j)])
        if s == 0:
            set_mod(g, l, 0, 1, 2, 0, 1, 0.5)
        else:
            set_mod(g, l, 6, 7, 8, 4, 5, 0.5)
        xTs = [k.sb(f"fx{i}", [128, 8, NT], F32, st) for i in range(2)]
        sqs = [k.sb(f"fsq{i}", [128, 8, NT], BF16, st) for i in range(2)]
        tmp = k.sb("ftmp", [128, 8, NT], F32, st)
        y = k.sb("fy", [128, 8, NT], F32, st)
        us = [k.sb(f"fu{i}", [128, 8, NT], BF16, st) for i in range(2)]
        hst = k.sb("fh", [128, 22, NT], BF16, st)
        rstds = [k.sb(f"frstd{i}", [128, NT], F32, st) for i in range(2)]
        sas = [k.sb(f"fsa{i}", [128, NT], F32, st) for i in range(2)]
        tiles = token_tiles(skip_ctx=(l == 3 and s == 1))
        load_tile(g, xTs[0], *tiles[0][:3])
        prenorm(g, xTs[0], tiles[0][2], tiles[0][3], us[0], sqs[0], tmp, rstds[0], PS[6])
        for ti, (b, t0, nt, col) in enumerate(tiles):
            xT = xTs[ti % 2]
            u = us[ti % 2]
            if ti + 1 < len(tiles):
                load_tile(g, xTs[(ti + 1) % 2], *tiles[ti + 1][:3])
            for j in range(22):
                pa = PS[(j % 2) * 2]; pb = PS[(j % 2) * 2 + 1]
                for c in range(8):
                    k.mm(pa[:, :nt], wA[:, c, j * 128:(j + 1) * 128], u[:, c, :nt], c == 0, c == 7, [(wA, c), (u, c)], [pa])
                for c in range(8):
                    k.mm(pb[:, :nt], wA[:, c, F + j * 128:F + (j + 1) * 128], u[:, c, :nt], c == 0, c == 7, [(wA, c), (u, c)], [pb])
                sa = sas[j % 2]
                k.act(sa[:, :nt], pa[:, :nt], AF.Silu, [pa], [sa])
                k.tt("dve", hst[:, j, :nt], sa[:, :nt], pb[:, :nt], ALU.mult, [sa, pb], [(hst, j)])
            if ti + 1 < len(tiles):
                nb_, nt0, nnt, ncol = tiles[ti + 1]
                prenorm(g, xTs[(ti + 1) % 2], nnt, ncol, us[(ti + 1) % 2], sqs[(ti + 1) % 2], tmp, rstds[(ti + 1) % 2], PS[7])
            for dc in range(8):
                py = PS[4 + dc % 2]
                for j in range(22):
                    k.mm(py[:, :nt], wB[:, j, dc * 128:(dc + 1) * 128], hst[:, j, :nt], j == 0, j == 21, [(wB, j), (hst, j)], [py])
                k.cp("act", y[:, dc, :nt], py[:, :nt], [py], [(y, dc)])
            postnorm_residual(g, y, sqs[ti % 2], nt, col, xT, rstds[ti % 2], PS[6])
            store_tile(g, xT, b, t0, nt)


NA_BW = 640 + 8 * 512


def mixer_prenorm_all(g, b, uT, st, skip=None):
    k = g.k
    with Scope(k) as s2:
        xTs = [k.sb(f"mx{i}", [128, 8, NT], F32, s2) for i in range(2)]
        sq = k.sb("msq", [128, 8, NT], BF16, s2)
        tmp = k.sb("mtmp", [128, 8, NT], F32, s2)
        rstd = k.sb("mrstd", [128, NT], F32, s2)
        tiles = [(b, 0, NT, 2)] + [(b, C + i * NT, NT, b) for i in range(S // NT)]
        load_tile(g, xTs[0], *tiles[0][:3])
        for ti, (_, t0, nt, col) in enumerate(tiles):
            if ti + 1 < len(tiles):
                load_tile(g, xTs[(ti + 1) % 2], *tiles[ti + 1][:3])
            prenorm(g, xTs[ti % 2], nt, col, uT, sq, tmp, rstd, g.PS[6], uoff=t0)


def mixer_outproj(g, b, aT, w_out_ap, w_out_T, skip_ctx):
    k = g.k
    PS = g.PS
    with Scope(k) as s2:
        wo = k.sb("wo", [128, 8, D], BF16, s2)
        for c in range(8):
            k.dma("pool", wo[:, c, :], w_out_ap[c * 128:(c + 1) * 128, :], reads=[w_out_T], writes=[(wo, c)])
        xTs = [k.sb(f"ox{i}", [128, 8, NT], F32, s2) for i in range(2)]
        sq = k.sb("osq", [128, 8, NT], BF16, s2)
        y = k.sb("oy", [128, 8, NT], F32, s2)
        rstd = k.sb("orstd", [128, NT], F32, s2)
        tiles = ([] if skip_ctx else [(b, 0, NT, 2)]) + [(b, C + i * NT, NT, b) for i in range(S // NT)]
        load_tile(g, xTs[0], *tiles[0][:3])
        for ti, (_, t0, nt, col) in enumerate(tiles):
            xT = xTs[ti % 2]
            if ti + 1 < len(tiles):
                load_tile(g, xTs[(ti + 1) % 2], *tiles[ti + 1][:3])
            for dc in range(8):
                py = PS[4 + dc % 2]
                for h in range(8):
                    k.mm(py[:, :nt], wo[:, h, dc * 128:(dc + 1) * 128], aT[:, h, t0:t0 + nt], h == 0, h == 7, [(wo, h), aT], [py])
                k.cp("act", y[:, dc, :nt], py[:, :nt], [py], [(y, dc)])
            postnorm_residual(g, y, sq, nt, col, xT, rstd, PS[6])
            store_tile(g, xT, b, t0, nt)


def da_phase(g, l):
    import math
    k = g.k
    PS = g.PS
    slot = l // 3
    last = (l == 3)
    lam_init = 0.8 - 0.6 * math.exp(-0.3 * l)
    set_mod(g, l, 3, 4, 5, 2, 3, 1.0)
    with Scope(k) as st:
        lrow = k.sb("lrow", [1, 256], F32, st)
        lbc = k.sb("lbc", [128, 4, 64], F32, st)
        lpr = k.sb("lpr", [128, 2, 64], F32, st)
        lsm = k.sb("lsm", [128, 2], F32, st)
        nlam = k.sb("nlam", [128, 1], F32, st)
        gsub = k.sb("gsub", [128, 1], F32, st)
        onesf = k.sb("onesf", [1, 128], F32, st)
        k.memset("dve", onesf[:], 1.0, [onesf])
        k.dma("sp", lrow[:], g.da_lambda[slot:slot + 1, :, :].rearrange("o a d -> o (a d)"), reads=[g.da_lambda], writes=[lrow])
        k.mm(PS[0][:, :256], onesf[:], lrow[:], True, True, [onesf, lrow], [PS[0]])
        k.cp("dve", lbc[:].rearrange("p a d -> p (a d)"), PS[0][:, :256], [PS[0]], [lbc])
        lv = lbc[:].rearrange("p (a two) d -> p a two d", two=2)
        k.tt("dve", lpr[:], lv[:, :, 0, :], lv[:, :, 1, :], ALU.mult, [lbc], [lpr])
        k.op("dve", lambda e: e.reduce_sum(lsm[:], lpr[:], axis=AX.X), [lpr], [lsm])
        k.act(lsm[:], lsm[:], AF.Exp, [lsm], [lsm])
        k.tt("dve", nlam[:], lsm[:, 1:2], lsm[:, 0:1], ALU.subtract, [lsm], [nlam])
        k.ts("dve", nlam[:], nlam[:], -float(lam_init), None, ALU.add, None, [nlam], [nlam])
        k.dma("sp", gsub[:], g.da_subln[slot].rearrange("(p o) -> p o", o=1), reads=[g.da_subln], writes=[gsub])
        k.ts("dve", gsub[:], gsub[:], float(1.0 - lam_init), None, ALU.mult, None, [gsub], [gsub])

        qT = k.sb("qT", [128, 8, TT], BF16, st)
        kT = k.sb("kT", [128, 8, TT], BF16, st)
        V = k.sb("Vt", [128, 18, D], BF16, st)
        uT = k.sb("uT", [128, 8, TT], BF16, st)
        for b in range(NBC):
            mixer_prenorm_all(g, b, uT, st)
            with Scope(k) as s2:
                wps = [k.sb(f"wp{i}", [128, 8, D], BF16, s2) for i in range(2)]
                rts = [k.sb(f"rt{i}", [128, 2, 512], F32, s2) for i in range(2)]
                t1s = [k.sb(f"t1{i}", [128, 512], F32, s2) for i in range(2)]
                t2s = [k.sb(f"t2{i}", [128, 512], F32, s2) for i in range(2)]
                n = 0
                for (dst, off) in ((qT, 0), (kT, D)):
                    w0, w1 = wps
                    for c in range(8):
                        k.dma("pool", w0[:, c, :], g.da_w_in[slot, c * 128:(c + 1) * 128, off:off + D], reads=[g.da_w_in], writes=[(w0, c)])
                        k.dma("pool", w1[:, c, :], g.da_w_sw[slot, c * 128:(c + 1) * 128, off:off + D], reads=[g.da_w_sw], writes=[(w1, c)])
                    for (t0, w) in [(0, 256)] + [(C + i * 512, 512) for i in range(4)]:
                        lat = t0 >= C
                        if lat:
                            rt = rts[n % 2]
                            k.dma("sp", rt[:], g.rope[:, :, t0 - C:t0 - C + 512].rearrange("a p t -> p a t"), reads=[g.rope], writes=[rt])
                        for h in range(8):
                            pq = PS[(h % 2) * 2]; pqs = PS[(h % 2) * 2 + 1]
                            for c in range(8):
                                k.mm(pq[:, :w], w0[:, c, h * 128:(h + 1) * 128], uT[:, c, t0:t0 + w], c == 0, c == 7, [(w0, c), uT], [pq])
                            if not lat:
                                k.cp("act", dst[:, h, t0:t0 + w], pq[:, :w], [pq], [(dst, h)])
                                continue
                            for c in range(8):
                                k.mm(pqs[:, :w], w1[:, c, h * 128:(h + 1) * 128], uT[:, c, t0:t0 + w], c == 0, c == 7, [(w1, c), uT], [pqs])
                            t1 = t1s[h % 2]; t2 = t2s[h % 2]
                            k.tt("dve", t1[:], pq[:], rt[:, 0, :], ALU.mult, [pq, rt], [t1])
                            k.tt("dve", t2[:], pqs[:], rt[:, 1, :], ALU.mult, [pqs, rt], [t2])
                            k.tt("dve", dst[:, h, t0:t0 + w], t1[:], t2[:], ALU.add, [t1, t2], [(dst, h)])
                        n += 1
                wv = wps[0]
                for c in range(8):
                    k.dma("pool", wv[:, c, :], g.da_w_in[slot, c * 128:(c + 1) * 128, 2 * D:3 * D], reads=[g.da_w_in], writes=[(wv, c)])
                for kc in range(18):
                    for half in range(2):
                        pv = PS[(kc % 2) * 2 + half]
                        for c in range(8):
                            k.mm(pv[:], uT[:, c, kc * 128:(kc + 1) * 128], wv[:, c, half * 512:(half + 1) * 512], c == 0, c == 7, [uT, (wv, c)], [pv])
                        k.cp("act" if half else "dve", V[:, kc, half * 512:(half + 1) * 512], pv[:], [pv], [(V, kc)])
            import os
            if os.environ.get("DA_DBG") and b == 0:
                k.dma("pool", g.dbg[1, 0:128, :], qT[:, 0, :], reads=[qT], writes=[g.dbg])
                k.dma("pool", g.dbg[1, 128:256, :], kT[:, 0, :], reads=[kT], writes=[g.dbg])
                k.dma("pool", g.dbg[1, 256:384, 0:1024], V[:, 3, :], reads=[V], writes=[g.dbg])
                k.dma("pool", g.dbg[1, 384:512, :], uT[:, 0, :], reads=[uT], writes=[g.dbg])
                g.skip_dbg_copy = True
                return
            aT = uT
            with Scope(k) as s2:
                E1s = [k.sb(f"E1{i}", [128, 512], BF16, s2) for i in range(3)]
                E2s = [k.sb(f"E2{i}", [128, 512], BF16, s2) for i in range(3)]
                Es1 = k.sb("Esum1", [128, 512], F32, s2); Es2 = k.sb("Esum2", [128, 512], F32, s2)
                onesF = k.sb("onesF", [128, 128], F32, s2)
                k.memset("dve", onesF[:], 1.0, [onesF])
                fr1 = k.sb("fr1", [128, 512], F32, s2); fr2 = k.sb("fr2", [128, 512], F32, s2)
                fo = k.sb("fo", [128, 512], F32, s2); fo2 = k.sb("fo2", [128, 512], F32, s2)
                fsq = k.sb("fsq", [128, 512], BF16, s2)
                qtiles = ([] if last else [(0, 256, 2)]) + [(C + i * 512, 512, 18) for i in range(4)]
                units = [(h, t0, w, nkc) for h in range(8) for (t0, w, nkc) in qtiles]
                O1, O2 = PS[6], PS[7]

                def fin1(w):
                    Z1, Z2 = PS[0], PS[1]
                    k.mm(Z1[:, :w], onesF[:], Es1[:, :w], True, True, [onesF, Es1], [Z1])
                    k.mm(Z2[:, :w], onesF[:], Es2[:, :w], True, True, [onesF, Es2], [Z2])
                    k.rcp(fr1[:, :w], Z1[:, :w], [Z1], [fr1])
                    k.rcp(fr2[:, :w], Z2[:, :w], [Z2], [fr2])
                    k.tt("dve", fo[:, :w], O1[:, :w], fr1[:, :w], ALU.mult, [O1, fr1], [fo])
                    k.tt("dve", fo2[:, :w], O2[:, :w], fr2[:, :w], ALU.mult, [O2, fr2], [fo2])

                def fin2(h, t0, w, ss):
                    k.stt("dve", fo[:, :w], fo2[:, :w], nlam[:, 0:1], fo[:, :w], ALU.mult, ALU.add, [fo2, nlam, fo], [fo])
                    k.op("act", lambda e: e.square(fsq[:, :w], fo[:, :w]), [fo], [fsq])
                    k.mm(ss[:, :w], g.ones[:], fsq[:, :w], True, True, [g.ones, fsq], [ss])
                    k.act(fr1[:, :w], ss[:, :w], AF.Sqrt, [ss], [fr1], bias=EPS, scale=1.0 / 128)
                    k.rcp(fr1[:, :w], fr1[:, :w], [fr1], [fr1])
                    k.tt("dve", fo[:, :w], fo[:, :w], fr1[:, :w], ALU.mult, [fo, fr1], [fo])
                    k.ts("dve", aT[:, h, t0:t0 + w], fo[:, :w], gsub[:, 0:1], None, ALU.mult, None, [fo, gsub], [(aT, h)])

                pend = None
                for (h, t0, w, nkc) in units:
                    def qk(kc):
                        s1 = PS[(kc % 3) * 2]; s2p = PS[(kc % 3) * 2 + 1]
                        k.mm(s1[:, :w], kT[0:64, h, kc * 128:(kc + 1) * 128], qT[0:64, h, t0:t0 + w], True, True, [(kT, h), (qT, h)], [s1])
                        k.mm(s2p[:, :w], kT[64:128, h, kc * 128:(kc + 1) * 128], qT[64:128, h, t0:t0 + w], True, True, [(kT, h), (qT, h)], [s2p])
                    qk(0)
                    if nkc > 1:
                        qk(1)
                    for kc in range(nkc):
                        s1 = PS[(kc % 3) * 2]; s2p = PS[(kc % 3) * 2 + 1]
                        E1 = E1s[kc % 3]; E2 = E2s[kc % 3]
                        k.act(E1[:, :w], s1[:, :w], AF.Exp, [s1], [E1], scale=0.125)
                        k.act(E2[:, :w], s2p[:, :w], AF.Exp, [s2p], [E2], scale=0.125)
                        if kc + 2 < nkc:
                            qk(kc + 2)
                        fst = kc == 0; lst = kc == nkc - 1
                        k.mm(O1[:, :w], V[:, kc, h * 128:(h + 1) * 128], E1[:, :w], fst, lst, [(V, kc), E1], [O1])
                        k.mm(O2[:, :w], V[:, kc, h * 128:(h + 1) * 128], E2[:, :w], fst, lst, [(V, kc), E2], [O2])
                        if fst:
                            k.cp("dve", Es1[:, :w], E1[:, :w], [E1], [Es1])
                            k.cp("dve", Es2[:, :w], E2[:, :w], [E2], [Es2])
                        else:
                            k.tt("dve", Es1[:, :w], Es1[:, :w], E1[:, :w], ALU.add, [Es1, E1], [Es1])
                            k.tt("dve", Es2[:, :w], Es2[:, :w], E2[:, :w], ALU.add, [Es2, E2], [Es2])
                        if pend is not None and kc == min(1, nkc - 1):
                            fin2(*pend, PS[(kc % 3) * 2 + 1])
                            pend = None
                    fin1(w)
                    pend = (h, t0, w)
                fin2(*pend, PS[3])
            if os.environ.get("DA_DBG3") and b == 0:
                for h in range(8):
                    k.dma("pool", g.dbg[1, h * 128:(h + 1) * 128, :], aT[:, h, :], reads=[aT], writes=[g.dbg])
                g.skip_dbg_copy = True
            mixer_outproj(g, b, aT, g.da_w_out[slot], g.da_w_out, last)


def na_chunks(r):
    rs = min(max(r - 4, 0), 24)
    out = []
    for m in range(rs // 2, (rs + 7) // 2 + 1):
        if 4 <= r <= 28:
            boff = (2 * m - r + 5) * 64
        elif r < 4:
            boff = 640 + r * 512 + (2 * m) * 64
        else:
            boff = 640 + (4 + r - 28) * 512 + (2 * m - 24) * 64
        out.append((2 + m, boff))
    return out


def na_phase(g, l):
    import os
    k = g.k
    PS = g.PS
    set_mod(g, l, 3, 4, 5, 2, 3, 1.0)
    with Scope(k) as st:
        qT = k.sb("nqT", [128, 8, TT], BF16, st)
        kT = k.sb("nkT", [128, 8, TT], BF16, st)
        Vt = k.sb("nVt", [128, 18, 16 * 65], BF16, st)
        uT = k.sb("nuT", [128, 8, TT], BF16, st)
        k.memset("pool", Vt[:], 1.0, [Vt])
        for b in range(NBC):
            mixer_prenorm_all(g, b, uT, st)
            with Scope(k) as s2:
                wps = [k.sb(f"nwp{i}", [128, 8, D], BF16, s2) for i in range(2)]
                for pi, (dst, off) in enumerate(((qT, 0), (kT, D))):
                    w0 = wps[pi]
                    for c in range(8):
                        k.dma("pool", w0[:, c, :], g.na_w_in[0, c * 128:(c + 1) * 128, off:off + D], reads=[g.na_w_in], writes=[(w0, c)])
                    for (t0, w) in [(0, 256)] + [(C + i * 512, 512) for i in range(4)]:
                        for h in range(8):
                            pq = PS[h % 4]
                            for c in range(8):
                                k.mm(pq[:, :w], w0[:, c, h * 128:(h + 1) * 128], uT[:, c, t0:t0 + w], c == 0, c == 7, [(w0, c), uT], [pq])
                            if pi == 0:
                                k.op("act", lambda e, o=dst[:, h, t0:t0 + w], i=pq[:, :w]: e.mul(o, i, 0.125), [pq], [(dst, h)])
                            else:
                                k.cp("dve", dst[:, h, t0:t0 + w], pq[:, :w], [pq], [(dst, h)])
                wv = wps[0]
                for c in range(8):
                    k.dma("pool", wv[:, c, :], g.na_w_in[0, c * 128:(c + 1) * 128, 2 * D:3 * D], reads=[g.na_w_in], writes=[(wv, c)])
                for kc in range(18):
                    for half in range(2):
                        pv = PS[(kc % 2) * 2 + half]
                        for c in range(8):
                            k.mm(pv[:], uT[:, c, kc * 128:(kc + 1) * 128], wv[:, c, half * 512:(half + 1) * 512], c == 0, c == 7, [uT, (wv, c)], [pv])
                        dest = Vt[:, kc, half * 520:(half + 1) * 520].rearrange("p (h e) -> p h e", e=65)[:, :, 0:64]
                        k.cp("act" if half else "dve", dest, pv[:].rearrange("p (h e) -> p h e", e=64), [pv], [(Vt, kc)])
            aT = uT
            with Scope(k) as s2:
                bts = [k.sb(f"nbt{i}", [128, NA_BW], BF16, s2) for i in range(2)]
                Es = [k.sb(f"nE{i}", [128, 512], BF16, s2) for i in range(4)]
                rzs = [k.sb(f"nrz{i}", [128, 2], F32, s2) for i in range(2)]
                ons = [k.sb(f"non{i}", [128, 128], F32, s2) for i in range(2)]

                def fin_a(po, nq, par):
                    rz, on = rzs[par], ons[par]
                    pov = po[:nq, 0:130].rearrange("p (h e) -> p h e", e=65)
                    k.rcp(rz[:nq, :], pov[:, :, 64], [po], [rz])
                    k.tt("dve", on[:nq, :].rearrange("p (h e) -> p h e", e=64), pov[:, :, 0:64],
                         rz[:nq, :].unsqueeze(2).to_broadcast([nq, 2, 64]), ALU.mult, [po, rz], [on])

                def fin_b(nq, c, tok0, pt, par):
                    on = ons[par]
                    k.tr(pt[:, :nq], on[:nq, :], g.ident[:nq, :nq], [on, g.ident], [pt])
                    k.cp("act", aT[:, c, tok0:tok0 + nq], pt[:, :nq], [pt], [(aT, c)])

                for c in range(8):
                    bt = bts[c % 2]
                    k.dma("pool", bt[0:64, :], g.na_bias[:, 2 * c, :], reads=[g.na_bias], writes=[bt])
                    k.dma("pool", bt[64:128, :], g.na_bias[:, 2 * c + 1, :], reads=[g.na_bias], writes=[bt])
                    for qh in range(2):
                        po = PS[4 + qh % 2]
                        for hh in range(2):
                            h = 2 * c + hh
                            pb = hh * 64
                            sp_ = PS[hh]
                            for j in range(2):
                                k.mm(sp_[:, j * 128:(j + 1) * 128], kT[pb:pb + 64, c, j * 128:(j + 1) * 128],
                                     qT[pb:pb + 64, c, qh * 128:(qh + 1) * 128], True, True, [(kT, c), (qT, c)], [sp_])
                            E = Es[hh]
                            k.act(E[:, :256], sp_[:, :256], AF.Exp, [sp_], [E])
                            for j in range(2):
                                k.mm(po[:, hh * 65:(hh + 1) * 65], E[:, j * 128:(j + 1) * 128], Vt[:, j, h * 65:(h + 1) * 65],
                                     j == 0, j == 1, [E, (Vt, j)], [po])
                        fin_a(po, 128, qh % 2)
                        fin_b(128, c, qh * 128, PS[6 + qh % 2], qh % 2)

                    def st_scores(r):
                        tq0 = C + r * 64
                        chunks = [(0, None), (1, None)] + na_chunks(r)
                        nch = len(chunks)
                        for hh in range(2):
                            pb = hh * 64
                            sp_ = PS[(r % 2) * 2 + hh]
                            for j, (kc, boff) in enumerate(chunks):
                                k.mm(sp_[:, j * 64:(j + 1) * 64], kT[pb:pb + 64, c, kc * 128:(kc + 1) * 128],
                                     qT[pb:pb + 64, c, tq0:tq0 + 64], True, boff is None, [(kT, c), (qT, c)], [sp_])
                                if boff is not None:
                                    k.mm(sp_[:, j * 64:(j + 1) * 64], bt[pb:pb + 64, boff:boff + 128], g.identb[pb:pb + 64, pb:pb + 64],
                                         False, True, [bt, g.identb], [sp_])
                        for hh in range(2):
                            sp_ = PS[(r % 2) * 2 + hh]
                            E = Es[(r % 2) * 2 + hh]
                            k.act(E[:, :nch * 64], sp_[:, :nch * 64], AF.Exp, [sp_], [E])

                    def st_pv(r):
                        chunks = [(0, None), (1, None)] + na_chunks(r)
                        nch = len(chunks)
                        po = PS[4 + r % 2]
                        for hh in range(2):
                            h = 2 * c + hh
                            E = Es[(r % 2) * 2 + hh]
                            for j, (kc, boff) in enumerate(chunks):
                                k.mm(po[0:64, hh * 65:(hh + 1) * 65], E[:, j * 64:(j + 1) * 64], Vt[:, kc, h * 65:(h + 1) * 65],
                                     j == 0, j == nch - 1, [E, (Vt, kc)], [po])
                        fin_a(po, 64, r % 2)

                    st_scores(0)
                    for r in range(32):
                        if r + 1 < 32:
                            st_scores(r + 1)
                        st_pv(r)
                        if r >= 1:
                            fin_b(64, c, C + (r - 1) * 64, PS[6 + (r - 1) % 2], (r - 1) % 2)
                    fin_b(64, c, C + 31 * 64, PS[6 + 31 % 2], 31 % 2)
            if os.environ.get("NA_DBG") and b == 0:
                for h in range(8):
                    k.dma("pool", g.dbg[1, h * 128:(h + 1) * 128, :], aT[:, h, :], reads=[aT], writes=[g.dbg])
                k.dma("pool", g.dbg[0, 0:128, :], qT[:, 0, :], reads=[qT], writes=[g.dbg])
                k.dma("pool", g.dbg[0, 128:256, :], kT[:, 0, :], reads=[kT], writes=[g.dbg])
                k.dma("pool", g.dbg[0, 256:384, 0:1040], Vt[:, 1, :], reads=[Vt], writes=[g.dbg])
                g.skip_dbg_copy = True
            mixer_outproj(g, b, aT, g.na_w_out[0], g.na_w_out, False)


RW_C0 = 0.6065306597126334
NCH = 36


def rw_phase(g, l):
    import os
    k = g.k
    PS = g.PS
    nc = k.nc
    set_mod(g, l, 3, 4, 5, 2, 3, 1.0)
    if not hasattr(g, "rw_scr"):
        d = {}
        d["NT"] = k.dram("rw_NT", [2, 2 * NCH * 16, 4096], F32)
        d["TTs"] = k.dram("rw_TT", [2, 2 * NCH * 16, 4096], BF16)
        for nm in ("ARB", "AAK", "ARK"):
            d[nm] = k.dram("rw_" + nm, [2, NBC, NCH, 16, 64, 64], BF16)
        d["TM"] = k.dram("rw_TM", [2, NBC, 3, TT, D], BF16)
        d["V"] = k.dram("rw_V", [NBC, TT, D], BF16)
        d["RT"] = k.dram("rw_RT", [2, NBC, D, TT], BF16)
        d["WT"] = k.dram("rw_WT", [2, NBC, D, NCH], F32)
        d["Y"] = k.dram("rw_Y", [2, NBC, TT, D], F32)
        d["BVG"] = k.dram("rw_BVG", [NBC, 2, TT, D], F32)
        g.rw_scr = d
    d = g.rw_scr
    A, ACT_, DVE, POOL = ALU, "act", "dve", "pool"

    with Scope(k) as st:
        cst = k.sb("rwc", [128, 5, 128], F32, st)
        k.dma("sp", cst[:], g.rwconst[:].rearrange("a p t -> p a t"), reads=[g.rwconst], writes=[cst])
        onesf = k.sb("ronesf", [1, 128], F32, st)
        k.memset(DVE, onesf[:], 1.0, [onesf])
        prow = k.sb("prow", [1, D], F32, st)
        bc = {}
        for nm, row in (("w00", V_W0), ("w01", V_W0 + 1), ("a00", V_A0), ("a01", V_A0 + 1), ("kk", V_KK), ("ka", V_KA), ("rk", V_RK)):
            t = k.sb("bc_" + nm, [128, D], F32, st)
            k.dma("sp", prow[:], g.vecs[row:row + 1, :], reads=[g.vecs], writes=[prow])
            for half in range(2):
                k.mm(PS[half][:], onesf[:], prow[:, half * 512:(half + 1) * 512], True, True, [onesf, prow], [PS[half]])
                k.cp(ACT_ if half else DVE, t[:, half * 512:(half + 1) * 512], PS[half][:], [PS[half]], [t])
            bc[nm] = t
        omka = k.sb("bc_omka", [128, D], F32, st)
        k.ts(DVE, omka[:], bc["ka"][:], -1.0, 1.0, A.mult, A.add, [bc["ka"]], [omka])
        win = k.sb("rwin", [128, 8, 3 * D], BF16, st)
        w1 = k.sb("rw1", [128, 8, 2, 64], BF16, st)
        a1 = k.sb("ra1", [128, 8, 2, 64], BF16, st)
        g1 = k.sb("rg1", [128, 8, 160], BF16, st)
        w2 = k.sb("rw2", [64, 2, D], BF16, st)
        a2 = k.sb("ra2", [64, 2, D], BF16, st)
        g2 = k.sb("rg2", [128, 2, D], BF16, st)
        for c in range(8):
            k.dma(POOL, win[:, c, :], g.rw_w_in[0, c * 128:(c + 1) * 128, :], reads=[g.rw_w_in], writes=[(win, c)])
        for z in range(2):
            k.dma(POOL, w1[:, :, z, :], g.rw_w1[0, z].rearrange("(c p) r -> p c r", p=128), reads=[g.rw_w1], writes=[w1])
            k.dma(POOL, a1[:, :, z, :], g.rw_a1[0, z].rearrange("(c p) r -> p c r", p=128), reads=[g.rw_a1], writes=[a1])
            k.dma(POOL, w2[:, z, :], g.rw_w2[0, z], reads=[g.rw_w2], writes=[w2])
            k.dma(POOL, a2[:, z, :], g.rw_a2[0, z], reads=[g.rw_a2], writes=[a2])
        k.dma(POOL, g1[:], g.rw_g1[0].rearrange("(c p) r -> p c r", p=128), reads=[g.rw_g1], writes=[g1])
        k.dma(POOL, g2[:, 0, :], g.rw_g2[0, 0:128, :], reads=[g.rw_g2], writes=[g2])
        k.dma(POOL, g2[0:32, 1, :], g.rw_g2[0, 128:160, :], reads=[g.rw_g2], writes=[g2])
        xT = k.sb("rxT", [128, 8, 130], F32, st)
        U = k.sb("rU", [128, 8, 130], F32, st)
        sq = k.sb("rsq", [128, 8, 130], BF16, st)
        tmp = k.sb("rtmp", [128, 8, 130], F32, st)
        rstd = k.sb("rrstd", [128, 130], F32, st)
        xj = [k.sb(f"rxj{j}", [128, 8, 128], BF16, st) for j in range(6)]
        hT = [k.sb(f"rhT{i}", [128, 128], BF16, st) for i in range(6)]
        Tt = [k.sb(f"rT{i}", [128, D], F32, st) for i in range(12)]
        FM = k.sb("rFM", [128, 8, 2, 4, 64], BF16, st)
        MMt = k.sb("rMM", [128, 16, 128], F32, st)
        WTt = k.sb("rWT", [128, 8, 2], F32, st)
        hsum = k.sb("rhs", [128, 16], F32, st)
        (Tr, Tk, Tv, Tkk, Tsg, Ta, Tkd, Tb, Tcs, Tx, Ty, Tks) = Tt
        for b in range(NBC):
            col_lat = b
            for ti in range(18):
                t0 = ti * 128
                col = 2 if t0 < C else col_lat
                seq0, seq1 = (0, C) if t0 < C else (C, TT)
                lo = max(t0 - 1, seq0)
                hi = min(t0 + 129, seq1)
                o0 = lo - (t0 - 1)
                nl = hi - lo
                k.dma("sp", xT[:, :, o0:o0 + nl], g.hT[b, :, lo:hi].rearrange("(c p) t -> p c t", p=128),
                      reads=hslots(g, b, lo, nl), writes=[xT])
                if o0 > 0:
                    k.memset(POOL, xT[:, :, 0:1], 1.0, [xT])
                if o0 + nl < 130:
                    k.memset(POOL, xT[:, :, 129:130], 1.0, [xT])
                prenorm(g, xT, 130, col, U, sq, tmp, rstd, PS[6])
                if o0 > 0:
                    k.memset(POOL, U[:, :, 0:1], 0.0, [U])
                if o0 + nl < 130:
                    k.memset(POOL, U[:, :, 129:130], 0.0, [U])
                xxv = xT[:, :, 0:128]
                k.tt(DVE, xxv, U[:, :, 0:128], U[:, :, 2:130], A.add, [U], [xT])
                k.stt(DVE, xxv, xxv, 0.5, U[:, :, 1:129], A.mult, A.subtract, [xT, U], [xT])
                for j in range(6):
                    mu = g.vp[:, V_MU + j, :].unsqueeze(2).to_broadcast([128, 8, 128])
                    k.tt(DVE, tmp[:, :, 0:128], xxv, mu, A.mult, [xT, g.vp], [tmp])
                    k.tt(DVE, xj[j][:], tmp[:, :, 0:128], U[:, :, 1:129], A.add, [tmp, U], [xj[j]])
                xr, xw, xk, xv, xa, xg = xj
                for (xs_, off, dst) in ((xr, 0, Tr), (xk, D, Tk), (xv, 2 * D, Tv)):
                    for half in range(2):
                        p = PS[half]
                        for c in range(8):
                            k.mm(p[:], xs_[:, c, :], win[:, c, off + half * 512: off + (half + 1) * 512], c == 0, c == 7, [xs_, (win, c)], [p])
                        k.cp(ACT_ if half else DVE, dst[:, half * 512:(half + 1) * 512], p[:], [p], [dst])
                k.dma(POOL, d["V"][b, t0:t0 + 128, :], Tv[:], reads=[Tv], writes=[d["V"]])
                for z in range(2):
                    p = PS[2]
                    for c in range(8):
                        k.mm(p[0:64, 0:128], w1[:, c, z, :], xw[:, c, :], c == 0, c == 7, [w1, xw], [p])
                    k.act(hT[z][0:64, :], p[0:64, 0:128], AF.Tanh, [p], [hT[z]])
                    p = PS[3]
                    for c in range(8):
                        k.mm(p[0:64, 0:128], a1[:, c, z, :], xa[:, c, :], c == 0, c == 7, [a1, xa], [p])
                    k.cp(DVE, hT[2 + z][0:64, :], p[0:64, 0:128], [p], [hT[2 + z]])
                p = PS[2]
                for c in range(8):
                    k.mm(p[:, 0:128], g1[:, c, 0:128], xg[:, c, :], c == 0, c == 7, [g1, xg], [p])
                k.act(hT[4][:, :], p[:, 0:128], AF.Sigmoid, [p], [hT[4]])
                p = PS[3]
                for c in range(8):
                    k.mm(p[0:32, 0:128], g1[:, c, 128:160], xg[:, c, :], c == 0, c == 7, [g1, xg], [p])
                k.act(hT[5][0:32, :], p[0:32, 0:128], AF.Sigmoid, [p], [hT[5]])
                for half in range(2):
                    p = PS[half]
                    k.mm(p[:], hT[4][:, :], g2[:, 0, half * 512:(half + 1) * 512], True, False, [hT[4], g2], [p])
                    k.mm(p[:], hT[5][0:32, :], g2[0:32, 1, half * 512:(half + 1) * 512], False, True, [hT[5], g2], [p])
                    k.cp(ACT_ if half else DVE, Tx[:, half * 512:(half + 1) * 512], p[:], [p], [Tx])
                k.dma("sp", d["BVG"][b, 1, t0:t0 + 128, :], Tx[:], reads=[Tx], writes=[d["BVG"]])
                k.tt(DVE, Tkk[:], Tk[:], bc["kk"][:], A.mult, [Tk, bc["kk"]], [Tkk])
                k.op(ACT_, lambda e: e.square(Ty[:], Tkk[:]), [Tkk], [Ty])
                k.op(DVE, lambda e: e.reduce_sum(hsum[:], Ty[:].rearrange("p (h e) -> p h e", e=64), axis=AX.X), [Ty], [hsum])
                k.act(hsum[:], hsum[:], AF.Sqrt, [hsum], [hsum])
                k.ts(DVE, hsum[:], hsum[:], 1e-12, None, A.max, None, [hsum], [hsum])
                k.rcp(hsum[:], hsum[:], [hsum], [hsum])
                k.tt(DVE, Tkk[:].rearrange("p (h e) -> p h e", e=64), Tkk[:].rearrange("p (h e) -> p h e", e=64),
                     hsum[:].unsqueeze(2).to_broadcast([128, 16, 64]), A.mult, [Tkk, hsum], [Tkk])
                for z in range(2):
                    zb = z * NBC + b
                    for half in range(2):
                        p = PS[half]
                        k.mm(p[:], hT[z][0:64, :], w2[:, z, half * 512:(half + 1) * 512], True, True, [hT[z], w2], [p])
                        k.tt(DVE, Tsg[:, half * 512:(half + 1) * 512], p[:], bc["w0%d" % z][:, half * 512:(half + 1) * 512], A.add, [p, bc["w0%d" % z]], [Tsg])
                        p = PS[2 + half]
                        k.mm(p[:], hT[2 + z][0:64, :], a2[:, z, half * 512:(half + 1) * 512], True, True, [hT[2 + z], a2], [p])
                        k.tt(DVE, Ta[:, half * 512:(half + 1) * 512], p[:], bc["a0%d" % z][:, half * 512:(half + 1) * 512], A.add, [p, bc["a0%d" % z]], [Ta])
                    k.act(Tsg[:], Tsg[:], AF.Sigmoid, [Tsg], [Tsg])
                    k.act(Ta[:], Ta[:], AF.Sigmoid, [Ta], [Ta])
                    k.tt(DVE, Tkd[:], Ta[:], bc["ka"][:], A.mult, [Ta, bc["ka"]], [Tkd])
                    k.tt(DVE, Tkd[:], Tkd[:], omka[:], A.add, [Tkd, omka], [Tkd])
                    k.tt(DVE, Tkd[:], Tkd[:], Tk[:], A.mult, [Tkd, Tk], [Tkd])
                    if z == 0:
                        k.cp(ACT_, Tks[:], Tkd[:], [Tkd], [Tks])
                    else:
                        k.tt(DVE, Tks[:], Tks[:], Tkd[:], A.add, [Tks, Tkd], [Tks])
                    k.tt(DVE, Tb[:], Tkk[:], Ta[:], A.mult, [Tkk, Ta], [Tb])
                    csP = (PS[0], PS[1]); totP = (PS[2], PS[3])
                    for half in range(2):
                        k.mm(csP[half][:], cst[:, 2 + z, :], Tsg[:, half * 512:(half + 1) * 512], True, True, [cst, Tsg], [csP[half]])
                        k.mm(totP[half][:], cst[:, 4, :], Tsg[:, half * 512:(half + 1) * 512], True, True, [cst, Tsg], [totP[half]])
                    for half in range(2):
                        hs = slice(half * 512, (half + 1) * 512)
                        k.cp(ACT_, Tcs[:, hs], csP[half][:], [csP[half]], [Tcs])
                        k.tt(DVE, Tx[:, hs], Tcs[:, hs], Tsg[:, hs], A.subtract, [Tcs, Tsg], [Tx])
                        k.act(Tx[:, hs], Tx[:, hs], AF.Exp, [Tx], [Tx], scale=-RW_C0)
                        k.tt(DVE, Ty[:, hs], totP[half][:], Tcs[:, hs], A.subtract, [totP[half], Tcs], [Ty])
                        k.act(Ty[:, hs], Ty[:, hs], AF.Exp, [Ty], [Ty], scale=-RW_C0)
                    k.stt(DVE, Tx[:], Tkk[:], -1.0, Tx[:], A.mult, A.mult, [Tkk, Tx], [Tx])
                    k.dma(POOL, d["TM"][z, b, 0, t0:t0 + 128, :], Tx[:], reads=[Tx], writes=[d["TM"]])
                    BWt = Tsg
                    k.tt(DVE, BWt[:], Tb[:], Ty[:], A.mult, [Tb, Ty], [BWt])
                    k.dma(POOL, d["TM"][z, b, 1, t0:t0 + 128, :], BWt[:], reads=[BWt], writes=[d["TM"]])
                    k.tt(DVE, Ty[:], Tkd[:], Ty[:], A.mult, [Tkd, Ty], [Ty])
                    k.dma(POOL, d["TM"][z, b, 2, t0:t0 + 128, :], Ty[:], reads=[Ty], writes=[d["TM"]])
                    Trt, Tbt, Tkt, TW = Ta, Tb, Tkd, Tcs
                    k.act(Ta[:], Tcs[:], AF.Exp, [Tcs], [Ta], scale=-RW_C0)
                    k.tt(DVE, Trt[:], Ta[:], Tr[:], A.mult, [Ta, Tr], [Trt])
                    k.act(BWt[:], Tcs[:], AF.Exp, [Tcs], [BWt], scale=RW_C0)
                    k.tt(DVE, Tbt[:], Tb[:], BWt[:], A.mult, [Tb, BWt], [Tbt])
                    k.tt(DVE, Tkt[:], Tkd[:], BWt[:], A.mult, [Tkd, BWt], [Tkt])
                    for half in range(2):
                        k.act(TW[:, half * 512:(half + 1) * 512], totP[half][:], AF.Exp, [totP[half]], [TW], scale=-RW_C0)
                    n = 0
                    for c in range(8):
                        for qi, src in enumerate((Tx, Trt, Tbt, Tkt)):
                            p = PS[4 + n % 4]
                            k.tr(p[:, 0:128], src[:, c * 128:(c + 1) * 128], g.ident[:], [src, g.ident], [p])
                            k.cp(ACT_ if n % 2 else DVE, FM[:, c, :, qi, :], p[:, 0:128].rearrange("p (a t) -> p a t", a=2), [p], [FM])
                            n += 1
                        p = PS[4 + n % 4]
                        k.tr(p[:, 0:128], TW[:, c * 128:(c + 1) * 128], g.ident[:], [TW, g.ident], [p])
                        k.cp(DVE, WTt[:, c, :], p[:, 0:128].rearrange("p (a t) -> p a t", a=2)[:, :, 0], [p], [WTt])
                        n += 1
                    k.dma("sp", d["WT"][z, b, :, 2 * ti:2 * ti + 2].rearrange("(c p) a -> p c a", p=128), WTt[:], reads=[WTt], writes=[d["WT"]])
                    for ch in range(2):
                        k.dma("act", d["RT"][z, b, :, t0 + ch * 64:t0 + (ch + 1) * 64].rearrange("(c p) t -> p c t", p=128),
                              FM[:, :, ch, 1, :], reads=[FM], writes=[d["RT"]])
                    n = 0
                    for ch in range(2):
                        for c in range(8):
                            for hh in range(2):
                                pb = hh * 64
                                p = PS[n % 4]
                                k.mm(p[:, 0:128], FM[pb:pb + 64, c, ch, 2:4, :].rearrange("p a t -> p (a t)"),
                                     FM[pb:pb + 64, c, ch, 0:2, :].rearrange("p a t -> p (a t)"), True, True, [FM], [p])
                                k.tt(DVE, MMt[:, 2 * c + hh, :], p[:, 0:128], cst[:, z, :], A.mult, [p, cst], [MMt])
                                n += 1
                        chunk = 2 * ti + ch
                        u0 = (b * NCH + chunk) * 16
                        k.dma("sp", d["NT"][z, u0:u0 + 16, :].rearrange("h (s t) -> s h t", t=64), MMt[0:64, :, 0:64], reads=[MMt], writes=[d["NT"]])
                        k.dma(POOL, d["ARB"][z, b, chunk].rearrange("h s t -> s h t"), MMt[0:64, :, 64:128], reads=[MMt], writes=[d["ARB"]])
                        k.dma(POOL, d["AAK"][z, b, chunk].rearrange("h s t -> s h t"), MMt[64:128, :, 0:64], reads=[MMt], writes=[d["AAK"]])
                        k.dma(POOL, d["ARK"][z, b, chunk].rearrange("h s t -> s h t"), MMt[64:128, :, 64:128], reads=[MMt], writes=[d["ARK"]])
                k.tt(DVE, Tx[:], Tr[:], Tks[:], A.mult, [Tr, Tks], [Tx])
                k.tt(DVE, Tx[:], Tx[:], bc["rk"][:], A.mult, [Tx, bc["rk"]], [Tx])
                k.op(DVE, lambda e: e.reduce_sum(hsum[:], Tx[:].rearrange("p (h e) -> p h e", e=64), axis=AX.X), [Tx], [hsum])
                k.tt(DVE, Tx[:].rearrange("p (h e) -> p h e", e=64), Tv[:].rearrange("p (h e) -> p h e", e=64),
                     hsum[:].unsqueeze(2).to_broadcast([128, 16, 64]), A.mult, [Tv, hsum], [Tx])
                k.dma("sp", d["BVG"][b, 0, t0:t0 + 128, :], Tx[:], reads=[Tx], writes=[d["BVG"]])
    stop = os.environ.get("RW_STOP", "")
    if stop == "A":
        return
    rw_solve(g)
    if stop == "B":
        return
    rw_scan(g)
    if stop == "C":
        return
    rw_readout(g)


def _diag(ap2d):
    return [ap2d[:, 0:4095].rearrange("p (a b) -> p a b", b=65)[:, :, 0], ap2d[:, 4095:4096]]


def rw_solve(g):
    k = g.k
    d = g.rw_scr
    with Scope(k) as st:
        NS = 3
        Ms = [k.sb(f"sM{i}", [128, 64, 64], F32, st) for i in range(NS)]
        Xs = [k.sb(f"sX{i}", [128, 64, 64], F32, st) for i in range(NS)]
        tps = [k.sb(f"sT{i}", [128, 64, 64], F32, st) for i in range(NS)]
        for z in range(2):
            for trip in range(3):
                grps = [trip * 3 + i for i in range(NS)]
                for i, grp in enumerate(grps):
                    M, X = Ms[i], Xs[i]
                    Mf = M[:].rearrange("p s t -> p (s t)")
                    Xf = X[:].rearrange("p s t -> p (s t)")
                    k.dma("sp", Mf, d["NT"][z, grp * 128:(grp + 1) * 128, :], reads=[d["NT"]], writes=[M])
                    for v in _diag(Mf):
                        k.memset("pool", v, 1.0, [M])
                    k.memset("pool", Xf, 0.0, [X])
                    for v in _diag(Xf):
                        k.memset("pool", v, 1.0, [X])
                rows = range(62, -1, -1) if z == 0 else range(1, 64)
                for s_ in rows:
                    lo, hi = (s_, 64) if z == 0 else (0, s_ + 1)
                    nt = hi - lo
                    for i in range(NS):
                        M, X, tp = Ms[i], Xs[i], tps[i]
                        k.tt("dve" if i == 2 else "pool", tp[:, 0:nt, 0:nt], X[:, lo:hi, lo:hi], M[:, s_, lo:hi].unsqueeze(2).to_broadcast([128, nt, nt]), ALU.mult, [X, M], [tp])
                        k.op("dve", lambda e, o=X[:, s_, lo:hi], i_=tp[:, 0:nt, 0:nt].rearrange("p t j -> p j t"): e.reduce_sum(o, i_, axis=AX.X), [tp], [X])
                for i, grp in enumerate(grps):
                    Xf = Xs[i][:].rearrange("p s t -> p (s t)")
                    k.dma("pool", d["TTs"][z, grp * 128:(grp + 1) * 128, :], Xf, reads=[Xs[i]], writes=[d["TTs"]])


def rw_scan(g):
    k = g.k
    PS = g.PS
    d = g.rw_scr

    def v3(p0, p1):
        return [p0[0:64, :].rearrange("p (h e) -> p h e", e=64), p1[0:64, :].rearrange("p (h e) -> p h e", e=64)]

    def stream(z, b, st, tag):
        def two(nm, shape, dt):
            return [k.sb(f"{nm}{tag}{i}", shape, dt, st) for i in range(2)]
        Tts = two("cTt", [64, 16, 64], BF16); ARBs = two("cARB", [64, 16, 64], BF16)
        AAKs = two("cAAK", [64, 16, 64], BF16); ARKs = two("cARK", [64, 16, 64], BF16)
        TMs = two("cTM", [64, 3, D], BF16); Vs = two("cV", [64, D], BF16); RTs = two("cRT", [64, 16, 64], BF16)
        Wc = k.sb("cW" + tag, [64, 16, NCH], F32, st)
        X0s = k.sb("cX0" + tag, [64, 16, 64], BF16, st); Ah = k.sb("cAh" + tag, [64, 16, 64], BF16, st)
        U0s = k.sb("cU0" + tag, [64, 16, 64], BF16, st); GTs = k.sb("cGT" + tag, [64, 16, 64], BF16, st)
        QTs = k.sb("cQT" + tag, [64, 16, 64], BF16, st); DWt = k.sb("cDW" + tag, [64, 16, 64], F32, st)
        Hb = k.sb("cHb" + tag, [64, 16, 64], BF16, st)
        ybs = two("cyb", [64, D], F32)
        qa, qb = ("sp", "act") if z == 0 else ("act", "sp")
        k.dma(qa, Wc[:], d["WT"][z, b].rearrange("(h q) c -> q h c", q=64), reads=[d["WT"]], writes=[Wc])
        k.memset("dve", Hb[:], 0.0, [Hb])
        order = list(range(NCH)) if z == 0 else [3, 2, 1, 0] + list(range(NCH - 1, 3, -1))

        def load(ci, i):
            ch = order[ci]
            u0 = (b * NCH + ch) * 16
            k.dma(qa, Tts[i][:], d["TTs"][z, u0:u0 + 16, :].rearrange("h (s t) -> s h t", t=64), reads=[d["TTs"]], writes=[Tts[i]])
            k.dma(qb, ARBs[i][:], d["ARB"][z, b, ch].rearrange("h s t -> s h t"), reads=[d["ARB"]], writes=[ARBs[i]])
            k.dma(qa, AAKs[i][:], d["AAK"][z, b, ch].rearrange("h s t -> s h t"), reads=[d["AAK"]], writes=[AAKs[i]])
            k.dma(qb, ARKs[i][:], d["ARK"][z, b, ch].rearrange("h s t -> s h t"), reads=[d["ARK"]], writes=[ARKs[i]])
            k.dma(qa, TMs[i][:], d["TM"][z, b, :, ch * 64:(ch + 1) * 64, :].rearrange("q t f -> t q f"), reads=[d["TM"]], writes=[TMs[i]])
            k.dma(qb, Vs[i][:], d["V"][b, ch * 64:(ch + 1) * 64, :], reads=[d["V"]], writes=[Vs[i]])
            k.dma(qa, RTs[i][:], d["RT"][z, b, :, ch * 64:(ch + 1) * 64].rearrange("(h q) t -> q h t", q=64), reads=[d["RT"]], writes=[RTs[i]])

        load(0, 0)
        hs = lambda h: slice(h * 64, (h + 1) * 64)
        pc = lambda P2, h: P2[h // 8][0:64, (h % 8) * 64:(h % 8 + 1) * 64]
        pz = 0 if z == 0 else 4
        for ci in range(NCH):
            i = ci % 2
            ch = order[ci]
            if ci + 1 < NCH:
                load(ci + 1, (ci + 1) % 2)
            Tt_, ARB_, AAK_, ARK_, TMc, Vc, RTc = Tts[i], ARBs[i], AAKs[i], ARKs[i], TMs[i], Vs[i], RTs[i]
            PX = (PS[(pz + 0) % 8], PS[(pz + 1) % 8])
            for h in range(16):
                k.mm(pc(PX, h), AAK_[:, h, :], Vc[:, hs(h)], True, True, [AAK_, Vc], [PX[h // 8]])
            for q, pv in enumerate(v3(*PX)):
                k.cp("act" if q else "dve", X0s[:, q * 8:(q + 1) * 8, :], pv, [PX[q]], [X0s])
            yield
            PA = (PS[(pz + 2) % 8], PS[(pz + 3) % 8]); PU = (PS[(pz + 4) % 8], PS[(pz + 5) % 8])
            for h in range(16):
                k.mm(pc(PA, h), Tt_[:, h, :], TMc[:, 0, hs(h)], True, True, [Tt_, TMc], [PA[h // 8]])
            for q, pv in enumerate(v3(*PA)):
                k.cp("act" if q else "dve", Ah[:, q * 8:(q + 1) * 8, :], pv, [PA[q]], [Ah])
            for h in range(16):
                k.mm(pc(PU, h), Tt_[:, h, :], X0s[:, h, :], True, True, [Tt_, X0s], [PU[h // 8]])
            for q, pv in enumerate(v3(*PU)):
                k.cp("act" if q else "dve", U0s[:, q * 8:(q + 1) * 8, :], pv, [PU[q]], [U0s])
            yield
            PG = (PS[(pz + 0) % 8], PS[(pz + 1) % 8]); PQ = (PS[(pz + 2) % 8], PS[(pz + 3) % 8])
            for h in range(16):
                k.mm(pc(PG, h), Ah[:, h, :], TMc[:, 1, hs(h)], True, True, [Ah, TMc], [PG[h // 8]])
            for h in range(16):
                k.mm(pc(PQ, h), Ah[:, h, :], ARB_[:, h, :], True, True, [Ah, ARB_], [PQ[h // 8]])
            k.tt("dve", DWt[:], g.ident[0:64, 0:64].unsqueeze(1).to_broadcast([64, 16, 64]),
                 Wc[:, :, ch].unsqueeze(2).to_broadcast([64, 16, 64]), ALU.mult, [g.ident, Wc], [DWt])
            for q, pv in enumerate(v3(*PG)):
                k.tt("dve", GTs[:, q * 8:(q + 1) * 8, :], pv, DWt[:, q * 8:(q + 1) * 8, :], ALU.add, [PG[q], DWt], [GTs])
            for q, pv in enumerate(v3(*PQ)):
                k.tt("dve", QTs[:, q * 8:(q + 1) * 8, :], pv, RTc[:, q * 8:(q + 1) * 8, :], ALU.add, [PQ[q], RTc], [QTs])
            yield
            PY = (PS[(pz + 6) % 8], PS[(pz + 7) % 8]); PH = (PS[(pz + 4) % 8], PS[(pz + 5) % 8])
            for h in range(16):
                k.mm(pc(PY, h), ARB_[:, h, :], U0s[:, h, :], True, False, [ARB_, U0s], [PY[h // 8]])
                k.mm(pc(PY, h), ARK_[:, h, :], Vc[:, hs(h)], False, False, [ARK_, Vc], [PY[h // 8]])
                k.mm(pc(PY, h), QTs[:, h, :], Hb[:, h, :], False, True, [QTs, Hb], [PY[h // 8]])
            for h in range(16):
                k.mm(pc(PH, h), TMc[:, 1, hs(h)], U0s[:, h, :], True, False, [TMc, U0s], [PH[h // 8]])
                k.mm(pc(PH, h), TMc[:, 2, hs(h)], Vc[:, hs(h)], False, False, [TMc, Vc], [PH[h // 8]])
                k.mm(pc(PH, h), GTs[:, h, :], Hb[:, h, :], False, True, [GTs, Hb], [PH[h // 8]])
            yb = ybs[i]
            for q in range(2):
                k.cp("act", yb[:, q * 512:(q + 1) * 512], PY[q][0:64, :], [PY[q]], [yb])
            for q, pv in enumerate(v3(*PH)):
                k.cp("dve", Hb[:, q * 8:(q + 1) * 8, :], pv, [PH[q]], [Hb])
            k.dma(qa, d["Y"][z, b, ch * 64:(ch + 1) * 64, :], yb[:], reads=[yb], writes=[d["Y"]])
            yield

    from itertools import zip_longest
    for b in range(NBC):
        with Scope(k) as st:
            gens = [stream(0, b, st, "a"), stream(1, b, st, "b")]
            for _ in zip_longest(*gens):
                pass


def rw_readout(g):
    k = g.k
    PS = g.PS
    d = g.rw_scr
    with Scope(k) as st:
        onesf = k.sb("eonesf", [1, 128], F32, st)
        k.memset("dve", onesf[:], 1.0, [onesf])
        prow = k.sb("eprow", [1, D], F32, st)
        bcs = []
        for nm, row in (("lng", V_LNG), ("lnb", V_LNB)):
            t = k.sb("ebc_" + nm, [128, D], F32, st)
            k.dma("sp", prow[:], g.vecs[row:row + 1, :], reads=[g.vecs], writes=[prow])
            for half in range(2):
                k.mm(PS[half][:], onesf[:], prow[:, half * 512:(half + 1) * 512], True, True, [onesf, prow], [PS[half]])
                k.cp("act" if half else "dve", t[:, half * 512:(half + 1) * 512], PS[half][:], [PS[half]], [t])
            bcs.append(t)
        lng, lnb = bcs
        aT = k.sb("eaT", [128, 8, TT], BF16, st)
        ys = [k.sb(f"ey{i}", [128, D], F32, st) for i in range(2)]
        y1 = k.sb("ey1", [128, D], F32, st); bv = k.sb("ebv", [128, D], F32, st); gg = k.sb("egg", [128, D], F32, st)
        sq = k.sb("esq", [128, D], F32, st)
        st1 = k.sb("est1", [128, 16], F32, st); st2 = k.sb("est2", [128, 16], F32, st)
        h3 = lambda t: t[:].rearrange("p (h e) -> p h e", e=64)
        hb = lambda t: t[:].unsqueeze(2).to_broadcast([128, 16, 64])
        for b in range(NBC):
            for ti in range(18):
                t0 = ti * 128
                y = ys[ti % 2]
                k.dma("sp", y[:], d["Y"][0, b, t0:t0 + 128, :], reads=[d["Y"]], writes=[y])
                k.dma("act", y1[:], d["Y"][1, b, t0:t0 + 128, :], reads=[d["Y"]], writes=[y1])
                k.dma("sp", bv[:], d["BVG"][b, 0, t0:t0 + 128, :], reads=[d["BVG"]], writes=[bv])
                k.dma("act", gg[:], d["BVG"][b, 1, t0:t0 + 128, :], reads=[d["BVG"]], writes=[gg])
                k.tt("dve", y[:], y[:], y1[:], ALU.add, [y, y1], [y])
                k.op("dve", lambda e, o=st1[:], i_=h3(y): e.reduce_sum(o, i_, axis=AX.X), [y], [st1])
                k.ts("dve", st1[:], st1[:], 1.0 / 64, None, ALU.mult, None, [st1], [st1])
                k.tt("dve", h3(y), h3(y), hb(st1), ALU.subtract, [y, st1], [y])
                k.op("act", lambda e, o=sq[:], i_=y[:]: e.square(o, i_), [y], [sq])
                k.op("dve", lambda e, o=st2[:], i_=h3(sq): e.reduce_sum(o, i_, axis=AX.X), [sq], [st2])
                k.act(st2[:], st2[:], AF.Sqrt, [st2], [st2], bias=64e-5, scale=1.0 / 64)
                k.rcp(st2[:], st2[:], [st2], [st2])
                k.tt("dve", h3(y), h3(y), hb(st2), ALU.mult, [y, st2], [y])
                k.tt("dve", y[:], y[:], lng[:], ALU.mult, [y, lng], [y])
                k.tt("dve", y[:], y[:], lnb[:], ALU.add, [y, lnb], [y])
                k.tt("dve", y[:], y[:], bv[:], ALU.add, [y, bv], [y])
                k.tt("dve", y[:], y[:], gg[:], ALU.mult, [y, gg], [y])
                for half in range(2):
                    p = PS[2 + half]
                    for q in range(4):
                        c = half * 4 + q
                        k.tr(p[:, q * 128:(q + 1) * 128], y[:, c * 128:(c + 1) * 128], g.ident[:], [y, g.ident], [p])
                    k.cp("act" if half else "dve", aT[:, half * 4:(half + 1) * 4, t0:t0 + 128], p[:].rearrange("p (q t) -> p q t", q=4), [p], [(aT, half)])
            mixer_outproj(g, b, aT, g.rw_w_out[0], g.rw_w_out, False)


def _host_tables():
    import math
    p = np.arange(128)
    axis = (p % 64) // 32
    half = (p % 32) // 16
    pair = p % 16
    t = np.arange(S)
    pos = np.stack([t // 64, t % 64], 0).astype(np.float32)
    freq = (10000.0 ** (-np.arange(16, dtype=np.float32) / 16)).astype(np.float32)
    ang = pos[axis, :] * freq[pair][:, None]
    rope = np.stack([np.cos(ang), np.where(half[:, None] == 0, -np.sin(ang), np.sin(ang))], 0).astype(np.float32)
    j = np.arange(2048)
    perm = (j // 32) * 32 + ((j % 32) + 16) % 32
    return rope, perm


def _na_bias_table(rpb):
    qc = np.arange(64)[:, None]
    kc = np.arange(64)[None, :]
    cs = np.clip(qc - 8, 0, 48)
    col_in = (kc >= cs) & (kc < cs + 16)
    off = np.clip(kc - qc + 15, 0, 30)
    def rows(drs):
        out = np.full((64, 16, len(drs), 64), -1e9, np.float32)
        for i, dr in enumerate(drs):
            if dr is None or dr < -7 or dr > 7:
                continue
            vals = rpb[:, dr + 7, :][:, off]
            out[:, :, i, :] = np.where(col_in[:, None, :], np.transpose(vals, (1, 0, 2)), -1e9)
        return out.reshape(64, 16, len(drs) * 64)
    strips = [rows([None] + list(range(-4, 4)) + [None])]
    for r in (0, 1, 2, 3):
        strips.append(rows([row - r for row in range(0, 8)]))
    for r in (28, 29, 30, 31):
        strips.append(rows([row - r for row in range(24, 32)]))
    return np.ascontiguousarray(np.concatenate(strips, axis=2))


def _rw_consts():
    i = np.arange(128)
    s_ = (i % 64)[:, None]
    t_ = (i % 64)[None, :]
    strict_col = (i < 64)[None, :]
    m0 = np.where(strict_col, t_ > s_, t_ >= s_)
    m1 = np.where(strict_col, t_ < s_, t_ <= s_)
    same = (i // 64)[:, None] == (i // 64)[None, :]
    tri0 = same & (i[:, None] <= i[None, :])
    tri1 = same & (i[:, None] >= i[None, :])
    return np.stack([m0, m1, tri0, tri1, same], 0).astype(np.float32)


def make_in_maps(inp):
    f = lambda a: np.ascontiguousarray(np.asarray(a, dtype=np.float32))
    rope, perm = _host_tables()
    vecs = np.concatenate([
        f(inp["norm_g"]).reshape(24, D), f(inp["ada_b"]).reshape(36, D),
        f(inp["rw_mu"])[0], f(inp["rw_w0"])[0], f(inp["rw_a0"])[0],
        f(inp["rw_k_k"]), f(inp["rw_k_a"]), f(inp["rw_ln_g"]), f(inp["rw_ln_b"]),
        f(inp["rw_r_k"]).reshape(1, D)], axis=0)
    assert vecs.shape == (NV, D)
    da_w_in = f(inp["da_w_in"])
    shared = {
        "vecs": np.ascontiguousarray(vecs), "ident": np.eye(128, dtype=np.float32),
        "ada_w": f(inp["ada_w"]), "ffn_w_in": f(inp["ffn_w_in"]), "ffn_w_out": f(inp["ffn_w_out"]),
        "da_w_in": da_w_in, "da_w_sw": np.ascontiguousarray(da_w_in[:, :, :2048][:, :, perm]),
        "da_w_out": f(inp["da_w_out"]), "da_lambda": f(inp["da_lambda"]), "da_subln_g": f(inp["da_subln_g"]),
        "rope": rope,
        "na_w_in": f(inp["na_w_in"]), "na_w_out": f(inp["na_w_out"]), "na_bias": _na_bias_table(f(inp["na_rpb"])[0]),
        "rw_w_in": f(inp["rw_w_in"]), "rw_w_out": f(inp["rw_w_out"]),
        "rw_w1": f(inp["rw_w1"]), "rw_w2": f(inp["rw_w2"]), "rw_a1": f(inp["rw_a1"]), "rw_a2": f(inp["rw_a2"]),
        "rw_g1": f(inp["rw_g1"]), "rw_g2": f(inp["rw_g2"]), "rwconst": _rw_consts(),
    }
    x = f(inp["x"]); ctx = f(inp["ctx"]); c = f(inp["c"]); cc = f(inp["c_ctx"])
    maps = []
    for i in range(8):
        m = dict(shared)
        m["x"] = np.ascontiguousarray(x[2 * i:2 * i + 2]); m["ctx"] = np.ascontiguousarray(ctx[2 * i:2 * i + 2])
        m["cvec"] = np.ascontiguousarray(np.concatenate([c[2 * i:2 * i + 2], cc[None, :]], 0))
        maps.append(m)
    return maps


def kernel(**inputs):
    nc = build_program()
    maps = make_in_maps(inputs)
    res = run_bass_kernel_spmd(nc, maps, core_ids=list(range(8)))
    return np.concatenate([np.asarray(r["out"], dtype=np.float32) for r in res.results], axis=0)
```

```python
import numpy as np
from contextlib import ExitStack
import concourse.bass as bass
import concourse.mybir as mybir

F32 = mybir.dt.float32
BF16 = mybir.dt.bfloat16
AF = mybir.ActivationFunctionType
ALU = mybir.AluOpType
AX = mybir.AxisListType


class T:
    __slots__ = ("h", "name", "lastw", "readers")

    def __init__(self, h, name):
        self.h = h
        self.name = name
        self.lastw = {}
        self.readers = {}

    def __getitem__(self, idx):
        return self.h[idx]


class K:
    def __init__(self, nc, n_dma_sems=20):
        self.nc = nc
        self.es = ExitStack()
        self.engs = {}
        for nm, h in (("pe", nc.tensor), ("act", nc.scalar), ("dve", nc.vector),
                      ("pool", nc.gpsimd), ("sp", nc.sync)):
            sem = self.es.enter_context(nc.semaphore("s_" + nm))
            self.engs[nm] = dict(h=h, sem=sem, cnt=0, waited={})
        self.dma_pool = {}
        for q in ("sp", "pool", "act"):
            sems = [self.es.enter_context(nc.semaphore(f"d_{q}{i}")) for i in range(n_dma_sems)]
            self.dma_pool[q] = dict(sems=sems, uses=[0] * n_dma_sems, nxt=0)
        self.semkey = {}
        self.phase_stack = None
        self.n_inst = 0

    def sb(self, name, shape, dt, stack=None):
        st = stack if stack is not None else self.es
        self.uid = getattr(self, "uid", 0) + 1
        name = f"{name}_{self.uid}"
        h = st.enter_context(self.nc.sbuf_tensor(name, list(shape), dt))
        return T(h, name)

    def ps(self, name, shape, dt, stack=None):
        st = stack if stack is not None else self.es
        h = st.enter_context(self.nc.psum_tensor(name, list(shape), dt))
        return T(h, name)

    def dram(self, name, shape, dt, kind="Internal"):
        h = self.nc.dram_tensor(name, list(shape), dt, kind=kind)
        return T(h, name)

    def _wait(self, eng, dep):
        sem, val, key = dep
        assert val is not None, "dependency on a non-incrementing instruction with no later incrementing one"
        e = self.engs[eng]
        if e["waited"].get(key, 0) >= val:
            return
        e["h"].wait_ge(sem, val)
        e["waited"][key] = val

    def _deps(self, reads, writes):
        deps = []
        for (t, s) in reads:
            for kk in ((s, None) if s is not None else tuple(t.lastw.keys())):
                d = t.lastw.get(kk)
                if d is not None:
                    deps.append(d)
        for (t, s) in writes:
            keys = (s, None) if s is not None else tuple(set(t.lastw.keys()) | set(t.readers.keys()))
            for kk in keys:
                d = t.lastw.get(kk)
                if d is not None:
                    deps.append(d)
                deps.extend(t.readers.get(kk, {}).values())
        return deps

    def _record(self, reads, writes, dep):
        for (t, s) in reads:
            r = t.readers.setdefault(s, {})
            old = r.get(dep[2])
            if old is None or dep[1] is None or (old[1] is not None and old[1] < dep[1]):
                r[dep[2]] = dep
        for (t, s) in writes:
            if s is None:
                t.lastw = {None: dep}
                t.readers = {}
            else:
                t.lastw[s] = dep
                t.readers[s] = {}

    @staticmethod
    def _norm(lst):
        out = []
        for x in lst:
            if isinstance(x, tuple):
                out.append(x)
            else:
                out.append((x, None))
        return out

    def op(self, eng, fn, reads=(), writes=(), inc=True):
        reads = self._norm(reads)
        writes = self._norm(writes)
        e = self.engs[eng]
        for d in self._deps(reads, writes):
            if eng == "pe" and d[2] == "pe":
                continue
            self._wait(eng, d)
        ins = fn(e["h"])
        import os
        if not os.environ.get("FW_LAZYINC"):
            inc = True
        if inc:
            e["cnt"] += 1
            ins.then_inc(e["sem"], 1)
            dep = [e["sem"], e["cnt"], eng]
            for p in e.setdefault("pending", []):
                p[1] = e["cnt"]
            e["pending"] = []
        else:
            dep = [e["sem"], None, eng]
            e.setdefault("pending", []).append(dep)
        self._record(reads, writes, dep)
        self.n_inst += 1
        return ins

    def dma(self, q, out_ap, in_ap, reads=(), writes=(), **kw):
        reads = self._norm(reads)
        writes = self._norm(writes)
        e = self.engs[q]
        p = self.dma_pool[q]
        i = p["nxt"]
        p["nxt"] = (i + 1) % len(p["sems"])
        sem = p["sems"][i]
        key = f"d_{q}{i}"
        if p["uses"][i] > 0:
            self._wait(q, (sem, 16 * p["uses"][i], key))
        for d in self._deps(reads, writes):
            self._wait(q, d)
        p["uses"][i] += 1
        ins = e["h"].dma_start(out=out_ap, in_=in_ap, **kw)
        ins.then_inc(sem, 16)
        dep = [sem, 16 * p["uses"][i], key]
        self._record(reads, writes, dep)
        self.n_inst += 1
        return dep

    def barrier(self):
        deps = []
        for nm, e in self.engs.items():
            if e["cnt"] > 0:
                deps.append((e["sem"], e["cnt"], nm))
        for q, p in self.dma_pool.items():
            for i, s in enumerate(p["sems"]):
                if p["uses"][i] > 0:
                    deps.append((s, 16 * p["uses"][i], f"d_{q}{i}"))
        for nm in self.engs:
            for d in deps:
                if d[2] == nm:
                    continue
                self._wait(nm, d)

    def finish(self):
        self.barrier()
        self.es.close()


def _mm(k, out, lhsT, rhs, start, stop, reads, writes, inc=None):
    return k.op("pe", lambda e: e.matmul(out, lhsT, rhs, start=start, stop=stop), reads, writes,
                inc=(stop if inc is None else inc))


def _tr(k, out, in_, ident, reads, writes):
    return k.op("pe", lambda e: e.transpose(out, in_, ident), reads, writes)


def _act(k, out, in_, func, reads, writes, bias=0.0, scale=1.0):
    return k.op("act", lambda e: e.activation(out, in_, func, bias=bias, scale=scale), reads, writes)


def _tt(k, eng, out, in0, in1, op, reads, writes):
    return k.op(eng, lambda e: e.tensor_tensor(out, in0, in1, op=op), reads, writes)


def _ts(k, eng, out, in0, s1, s2, op0, op1, reads, writes):
    if op1 is None:
        return k.op(eng, lambda e: e.tensor_scalar(out, in0, s1, None, op0=op0), reads, writes)
    return k.op(eng, lambda e: e.tensor_scalar(out, in0, s1, s2, op0=op0, op1=op1), reads, writes)


def _stt(k, eng, out, in0, scalar, in1, op0, op1, reads, writes):
    return k.op(eng, lambda e: e.scalar_tensor_tensor(out=out, in0=in0, scalar=scalar, in1=in1, op0=op0, op1=op1), reads, writes)


def _cp(k, eng, out, in_, reads, writes):
    if eng == "act":
        return k.op("act", lambda e: e.copy(out, in_), reads, writes)
    return k.op(eng, lambda e: e.tensor_copy(out, in_), reads, writes)


def _rcp(k, out, in_, reads, writes):
    return k.op("dve", lambda e: e.reciprocal(out, in_), reads, writes)


def _memset(k, eng, ap, val, writes):
    return k.op(eng, lambda e: e.memset(ap, val), (), writes)


K.mm = _mm; K.tr = _tr; K.act = _act; K.tt = _tt; K.ts = _ts; K.stt = _stt; K.cp = _cp; K.rcp = _rcp; K.memset = _memset

from concourse.bass_utils import run_bass_kernel_spmd

D = 1024
C = 256
S = 2048
TT = C + S
F = 2816
NBC = 2
EPS = 1e-6
NT = 256

V_NORMG = 0
V_ADAB = 24
V_RW = 60
(V_MU, V_W0, V_A0, V_KK, V_KA, V_LNG, V_LNB, V_RK) = (60, 66, 68, 70, 71, 72, 73, 74)
NV = 75


class Ctx:
    pass


class Scope:
    def __init__(self, k):
        self.k = k
        self.st = ExitStack()

    def __enter__(self):
        self.st.__enter__()
        return self.st

    def __exit__(self, *a):
        if a[0] is None:
            self.k.barrier()
        return self.st.__exit__(*a)


def hslots(g, b, t0, nt):
    return [(g.hT, (b, i)) for i in range(t0 // NT, (t0 + nt + NT - 1) // NT)]


def build_program(stop_after=None, dbg=False):
    nc = bass.Bass("TRN2", target_bir_lowering=False)
    k = K(nc)
    g = Ctx()
    g.k = k
    def ein(name, shape):
        return k.dram(name, shape, F32, kind="ExternalInput")
    g.x = ein("x", [NBC, S, D]); g.ctx = ein("ctx", [NBC, C, D]); g.cvec = ein("cvec", [3, D])
    g.vecs = ein("vecs", [NV, D]); g.identd = ein("ident", [128, 128])
    g.ada_w = ein("ada_w", [4, D, 9 * D])
    g.ffn_w_in = ein("ffn_w_in", [4, 2, D, 2 * F]); g.ffn_w_out = ein("ffn_w_out", [4, 2, F, D])
    g.da_w_in = ein("da_w_in", [2, D, 3 * D]); g.da_w_sw = ein("da_w_sw", [2, D, 2 * D]); g.da_w_out = ein("da_w_out", [2, D, D])
    g.da_lambda = ein("da_lambda", [2, 4, 64]); g.da_subln = ein("da_subln_g", [2, 128])
    g.rope = ein("rope", [2, 128, S])
    g.na_w_in = ein("na_w_in", [1, D, 3 * D]); g.na_w_out = ein("na_w_out", [1, D, D]); g.na_bias = ein("na_bias", [64, 16, NA_BW])
    g.rw_w_in = ein("rw_w_in", [1, D, 3 * D]); g.rw_w_out = ein("rw_w_out", [1, D, D])
    g.rw_w1 = ein("rw_w1", [1, 2, D, 64]); g.rw_w2 = ein("rw_w2", [1, 2, 64, D])
    g.rw_a1 = ein("rw_a1", [1, 2, D, 64]); g.rw_a2 = ein("rw_a2", [1, 2, 64, D])
    g.rw_g1 = ein("rw_g1", [1, D, 160]); g.rw_g2 = ein("rw_g2", [1, 160, D])
    g.rwconst = ein("rwconst", [5, 128, 128])
    g.out = k.dram("out", [NBC, S, D], F32, kind="ExternalOutput")
    g.hT = k.dram("hT", [NBC, D, TT], F32)
    if dbg:
        g.dbg = k.dram("dbg", [NBC, D, TT], F32, kind="ExternalOutput")

    g.ident = k.sb("identf", [128, 128], F32)
    g.identb = k.sb("identb", [128, 128], BF16)
    g.ones = k.sb("onesb", [128, 128], BF16)
    g.vp = k.sb("vp", [128, NV, 8], F32)
    g.mod = k.sb("mod", [128, 36, 8, 3], F32)
    g.mA = k.sb("mA", [128, 8, 3], F32); g.mB = k.sb("mB", [128, 8, 3], F32); g.mC = k.sb("mC", [128, 8, 3], F32)
    g.PS = [k.ps(f"ps{i}", [128, 512], F32) for i in range(8)]
    k.dma("sp", g.ident[:], g.identd[:], reads=[g.identd], writes=[g.ident])
    k.cp("dve", g.identb[:], g.ident[:], [g.ident], [g.identb])
    k.memset("dve", g.ones[:], 1.0, [g.ones])

    prologue(g)
    done = False
    for l in range(4):
        for ph in ("ffn0", "mix", "ffn1"):
            if done:
                break
            k.barrier()
            if ph == "ffn0":
                ffn_phase(g, l, 0)
            elif ph == "ffn1":
                ffn_phase(g, l, 1)
            else:
                if l % 3 == 0:
                    da_phase(g, l)
                elif l % 3 == 1:
                    na_phase(g, l)
                else:
                    rw_phase(g, l)
            if stop_after == (l, ph):
                done = True
    k.barrier()
    epilogue(g, dbg)
    k.finish()
    return nc


def prologue(g):
    k = g.k
    PS = g.PS
    with Scope(k) as st:
        vrow = k.sb("vrow", [NV, D], F32, st)
        crow = k.sb("crow", [3, D], F32, st)
        scT = k.sb("scT", [128, 8, 3], F32, st)
        k.dma("sp", vrow[:], g.vecs[:], reads=[g.vecs], writes=[vrow])
        k.dma("sp", crow[:], g.cvec[:], reads=[g.cvec], writes=[crow])
        k.act(crow[:], crow[:], AF.Silu, [crow], [crow])
        for dc in range(8):
            p = PS[dc % 2]
            k.tr(p[:, :NV], vrow[:, dc * 128:(dc + 1) * 128], g.ident[:NV, :NV], [vrow, g.ident], [p])
            k.cp("dve", g.vp[:, :, dc], p[:, :NV], [p], [g.vp])
            p2 = PS[2 + dc % 2]
            k.tr(p2[:, :3], crow[:, dc * 128:(dc + 1) * 128], g.ident[:3, :3], [crow, g.ident], [p2])
            k.cp("dve", scT[:, dc, :], p2[:, :3], [p2], [scT])
        aws = [k.sb(f"aw{i}", [128, 8, D], F32, st) for i in range(2)]
        xin = [k.sb(f"xin{i}", [128, D], F32, st) for i in range(2)]
        xst = [k.sb(f"xst{i}", [128, 8, 512], F32, st) for i in range(2)]

        def xpose_jobs():
            n = 0
            for b in range(NBC):
                groups = [(g.ctx, 0, 0, 256)] + [(g.x, i * 512, C + i * 512, 512) for i in range(4)]
                for gi, (src, s0, t0, w) in enumerate(groups):
                    xs = xst[gi % 2]
                    for sub in range(w // 128):
                        xi = xin[n % 2]
                        k.dma("sp", xi[:], src[b, s0 + sub * 128: s0 + (sub + 1) * 128, :], reads=[src], writes=[xi])
                        for half in range(2):
                            p = PS[(n % 2) * 2 + half]
                            for q in range(4):
                                dc = half * 4 + q
                                k.tr(p[:, q * 128:(q + 1) * 128], xi[:, dc * 128:(dc + 1) * 128], g.ident[:], [xi, g.ident], [p])
                            k.cp("act" if half else "dve", xs[:, half * 4:(half + 1) * 4, sub * 128:(sub + 1) * 128],
                                 p[:].rearrange("p (q t) -> p q t", q=4), [p], [xs])
                        n += 1
                        if sub == w // 128 - 1:
                            k.dma("act", g.hT[b, :, t0:t0 + w].rearrange("(c p) t -> p c t", p=128), xs[:, :, :w],
                                  reads=[xs], writes=hslots(g, b, t0, w))
                        yield

        jobs = xpose_jobs()
        n = 0
        for l in range(4):
            for j in range(9):
                aw = aws[n % 2]
                k.dma("sp" if n % 2 == 0 else "act", aw[:],
                      g.ada_w[l, :, j * D:(j + 1) * D].rearrange("(c p) f -> p c f", p=128),
                      reads=[g.ada_w], writes=[aw])
                next(jobs, None)
                pm = PS[4 + n % 2]
                for dc in range(8):
                    for cc in range(8):
                        k.mm(pm[:, dc * 3:(dc + 1) * 3], aw[:, cc, dc * 128:(dc + 1) * 128], scT[:, cc, :],
                             cc == 0, cc == 7, [aw, scT], [pm])
                k.tt("dve", g.mod[:, l * 9 + j, :, :], pm[:, 0:24].rearrange("p (c o) -> p c o", o=3),
                     g.vp[:, V_ADAB + l * 9 + j, :].unsqueeze(2).to_broadcast([128, 8, 3]), ALU.add,
                     [pm, g.vp], [g.mod])
                n += 1
        for _ in jobs:
            pass


def epilogue(g, dbg):
    k = g.k
    PS = g.PS
    with Scope(k) as st:
        xs2 = [k.sb(f"exs{i}", [128, 8, 512], F32, st) for i in range(2)]
        ot = [k.sb(f"eot{i}", [128, D], F32, st) for i in range(2)]
        n = 0
        for b in range(NBC):
            for gi in range(4):
                t0 = C + gi * 512
                xs = xs2[gi % 2]
                k.dma("sp", xs[:], g.hT[b, :, t0:t0 + 512].rearrange("(c p) t -> p c t", p=128),
                      reads=hslots(g, b, t0, 512), writes=[xs])
                for sub in range(4):
                    o = ot[n % 2]
                    for half in range(2):
                        p = PS[(n % 2) * 2 + half]
                        for q in range(4):
                            dc = half * 4 + q
                            k.tr(p[:, q * 128:(q + 1) * 128], xs[:, dc, sub * 128:(sub + 1) * 128], g.ident[:], [xs, g.ident], [p])
                        k.cp("act" if half else "dve", o[:, half * 512:(half + 1) * 512], p[:], [p], [o])
                    k.dma("act", g.out[b, gi * 512 + sub * 128: gi * 512 + (sub + 1) * 128, :], o[:], reads=[o], writes=[g.out])
                    n += 1
        if dbg and not getattr(g, "skip_dbg_copy", False):
            for b in range(NBC):
                k.dma("sp", g.dbg[b], g.hT[b], reads=[g.hT], writes=[g.dbg])


def set_mod(g, l, j_shift, j_scale, j_gate, s_pre, s_post, gate_mul):
    k = g.k
    def gv(s):
        return g.vp[:, V_NORMG + l * 6 + s, :].unsqueeze(2).to_broadcast([128, 8, 3])
    k.ts("dve", g.mA[:], g.mod[:, l * 9 + j_scale, :, :], 1.0, None, ALU.add, None, [g.mod], [g.mA])
    k.tt("dve", g.mA[:], g.mA[:], gv(s_pre), ALU.mult, [g.mA, g.vp], [g.mA])
    k.cp("dve", g.mB[:], g.mod[:, l * 9 + j_shift, :, :], [g.mod], [g.mB])
    k.stt("dve", g.mC[:], g.mod[:, l * 9 + j_gate, :, :], float(gate_mul), gv(s_post), ALU.mult, ALU.mult, [g.mod, g.vp], [g.mC])


def prenorm(g, xT, nt, col, u, sq, tmp, rstd, ps, uoff=0):
    k = g.k
    k.tt("dve", sq[:, :, :nt], xT[:, :, :nt], xT[:, :, :nt], ALU.mult, [xT], [sq])
    for c in range(8):
        k.mm(ps[:, :nt], g.ones[:], sq[:, c, :nt], c == 0, c == 7, [g.ones, sq], [ps])
    k.act(rstd[:, :nt], ps[:, :nt], AF.Sqrt, [ps], [rstd], bias=EPS, scale=1.0 / D)
    k.rcp(rstd[:, :nt], rstd[:, :nt], [rstd], [rstd])
    for c in range(8):
        k.stt("dve", tmp[:, c, :nt], xT[:, c, :nt], g.mA[:, c, col:col + 1], rstd[:, :nt], ALU.mult, ALU.mult,
              [xT, g.mA, rstd], [(tmp, c)])
        k.op("act", lambda e, o=u[:, c, uoff:uoff + nt], i_=tmp[:, c, :nt], a_=g.mB[:, c, col:col + 1]: e.add(o, i_, a_),
             [(tmp, c), g.mB], [(u, c)])


def postnorm_residual(g, y, sq, nt, col, xT, rstd, ps):
    k = g.k
    k.op("act", lambda e: e.square(sq[:, :, :nt], y[:, :, :nt]), [y], [sq])
    for c in range(8):
        k.mm(ps[:, :nt], g.ones[:], sq[:, c, :nt], c == 0, c == 7, [g.ones, sq], [ps])
    k.act(rstd[:, :nt], ps[:, :nt], AF.Sqrt, [ps], [rstd], bias=EPS, scale=1.0 / D)
    k.rcp(rstd[:, :nt], rstd[:, :nt], [rstd], [rstd])
    for c in range(8):
        k.tt("dve", y[:, c, :nt], y[:, c, :nt], rstd[:, :nt], ALU.mult, [(y, c), rstd], [(y, c)])
        k.stt("dve", xT[:, c, :nt], y[:, c, :nt], g.mC[:, c, col:col + 1], xT[:, c, :nt], ALU.mult, ALU.add,
              [(y, c), g.mC, (xT, c)], [(xT, c)])


def token_tiles(skip_ctx=False):
    tl = []
    for b in range(NBC):
        if not skip_ctx:
            tl.append((b, 0, NT, 2))
        for i in range(S // NT):
            tl.append((b, C + i * NT, NT, b))
    return tl


def load_tile(g, xT, b, t0, nt):
    g.k.dma("sp", xT[:, :, :nt], g.hT[b, :, t0:t0 + nt].rearrange("(c p) t -> p c t", p=128),
            reads=hslots(g, b, t0, nt), writes=[xT])


def store_tile(g, xT, b, t0, nt):
    g.k.dma("sp", g.hT[b, :, t0:t0 + nt].rearrange("(c p) t -> p c t", p=128), xT[:, :, :nt],
            reads=[xT], writes=hslots(g, b, t0, nt))


def ffn_phase(g, l, s):
    k = g.k
    PS = g.PS
    with Scope(k) as st:
        wA = k.sb("wA", [128, 8, 2 * F], BF16, st)
        wB = k.sb("wB", [128, 22, D], BF16, st)
        for c in range(8):
            k.dma("pool", wA[:, c, :], g.ffn_w_in[l, s, c * 128:(c + 1) * 128, :], reads=[g.ffn_w_in], writes=[(wA, c)])
        for j in range(22):
            k.dma("pool", wB[:, j, :], g.ffn_w_out[l, s, j * 128:(j + 1) * 128, :], reads=[g.ffn_w_out], writes=[(wB, j)])
        if s == 0:
            set_mod(g, l, 0, 1, 2, 0, 1, 0.5)
        else:
            set_mod(g, l, 6, 7, 8, 4, 5, 0.5)
        xTs = [k.sb(f"fx{i}", [128, 8, NT], F32, st) for i in range(2)]
        sqs = [k.sb(f"fsq{i}", [128, 8, NT], BF16, st) for i in range(2)]
        tmp = k.sb("ftmp", [128, 8, NT], F32, st)
        y = k.sb("fy", [128, 8, NT], F32, st)
        us = [k.sb(f"fu{i}", [128, 8, NT], BF16, st) for i in range(2)]
        hst = k.sb("fh", [128, 22, NT], BF16, st)
        rstds = [k.sb(f"frstd{i}", [128, NT], F32, st) for i in range(2)]
        sas = [k.sb(f"fsa{i}", [128, NT], F32, st) for i in range(2)]
        tiles = token_tiles(skip_ctx=(l == 3 and s == 1))
        load_tile(g, xTs[0], *tiles[0][:3])
        prenorm(g, xTs[0], tiles[0][2], tiles[0][3], us[0], sqs[0], tmp, rstds[0], PS[6])
        for ti, (b, t0, nt, col) in enumerate(tiles):
            xT = xTs[ti % 2]
            u = us[ti % 2]
            if ti + 1 < len(tiles):
                load_tile(g, xTs[(ti + 1) % 2], *tiles[ti + 1][:3])
            for j in range(22):
                pa = PS[(j % 2) * 2]; pb = PS[(j % 2) * 2 + 1]
                for c in range(8):
                    k.mm(pa[:, :nt], wA[:, c, j * 128:(j + 1) * 128], u[:, c, :nt], c == 0, c == 7, [(wA, c), (u, c)], [pa])
                for c in range(8):
                    k.mm(pb[:, :nt], wA[:, c, F + j * 128:F + (j + 1) * 128], u[:, c, :nt], c == 0, c == 7, [(wA, c), (u, c)], [pb])
                sa = sas[j % 2]
                k.act(sa[:, :nt], pa[:, :nt], AF.Silu, [pa], [sa])
                k.tt("dve", hst[:, j, :nt], sa[:, :nt], pb[:, :nt], ALU.mult, [sa, pb], [(hst, j)])
            if ti + 1 < len(tiles):
                nb_, nt0, nnt, ncol = tiles[ti + 1]
                prenorm(g, xTs[(ti + 1) % 2], nnt, ncol, us[(ti + 1) % 2], sqs[(ti + 1) % 2], tmp, rstds[(ti + 1) % 2], PS[7])
            for dc in range(8):
                py = PS[4 + dc % 2]
                for j in range(22):
                    k.mm(py[:, :nt], wB[:, j, dc * 128:(dc + 1) * 128], hst[:, j, :nt], j == 0, j == 21, [(wB, j), (hst, j)], [py])
                k.cp("act", y[:, dc, :nt], py[:, :nt], [py], [(y, dc)])
            postnorm_residual(g, y, sqs[ti % 2], nt, col, xT, rstds[ti % 2], PS[6])
            store_tile(g, xT, b, t0, nt)


NA_BW = 640 + 8 * 512


def mixer_prenorm_all(g, b, uT, st, skip=None):
    k = g.k
    with Scope(k) as s2:
        xTs = [k.sb(f"mx{i}", [128, 8, NT], F32, s2) for i in range(2)]
        sq = k.sb("msq", [128, 8, NT], BF16, s2)
        tmp = k.sb("mtmp", [128, 8, NT], F32, s2)
        rstd = k.sb("mrstd", [128, NT], F32, s2)
        tiles = [(b, 0, NT, 2)] + [(b, C + i * NT, NT, b) for i in range(S // NT)]
        load_tile(g, xTs[0], *tiles[0][:3])
        for ti, (_, t0, nt, col) in enumerate(tiles):
            if ti + 1 < len(tiles):
                load_tile(g, xTs[(ti + 1) % 2], *tiles[ti + 1][:3])
            prenorm(g, xTs[ti % 2], nt, col, uT, sq, tmp, rstd, g.PS[6], uoff=t0)


def mixer_outproj(g, b, aT, w_out_ap, w_out_T, skip_ctx):
    k = g.k
    PS = g.PS
    with Scope(k) as s2:
        wo = k.sb("wo", [128, 8, D], BF16, s2)
        for c in range(8):
            k.dma("pool", wo[:, c, :], w_out_ap[c * 128:(c + 1) * 128, :], reads=[w_out_T], writes=[(wo, c)])
        xTs = [k.sb(f"ox{i}", [128, 8, NT], F32, s2) for i in range(2)]
        sq = k.sb("osq", [128, 8, NT], BF16, s2)
        y = k.sb("oy", [128, 8, NT], F32, s2)
        rstd = k.sb("orstd", [128, NT], F32, s2)
        tiles = ([] if skip_ctx else [(b, 0, NT, 2)]) + [(b, C + i * NT, NT, b) for i in range(S // NT)]
        load_tile(g, xTs[0], *tiles[0][:3])
        for ti, (_, t0, nt, col) in enumerate(tiles):
            xT = xTs[ti % 2]
            if ti + 1 < len(tiles):
                load_tile(g, xTs[(ti + 1) % 2], *tiles[ti + 1][:3])
            for dc in range(8):
                py = PS[4 + dc % 2]
                for h in range(8):
                    k.mm(py[:, :nt], wo[:, h, dc * 128:(dc + 1) * 128], aT[:, h, t0:t0 + nt], h == 0, h == 7, [(wo, h), aT], [py])
                k.cp("act", y[:, dc, :nt], py[:, :nt], [py], [(y, dc)])
            postnorm_residual(g, y, sq, nt, col, xT, rstd, PS[6])
            store_tile(g, xT, b, t0, nt)


def da_phase(g, l):
    import math
    k = g.k
    PS = g.PS
    slot = l // 3
    last = (l == 3)
    lam_init = 0.8 - 0.6 * math.exp(-0.3 * l)
    set_mod(g, l, 3, 4, 5, 2, 3, 1.0)
    with Scope(k) as st:
        lrow = k.sb("lrow", [1, 256], F32, st)
        lbc = k.sb("lbc", [128, 4, 64], F32, st)
        lpr = k.sb("lpr", [128, 2, 64], F32, st)
        lsm = k.sb("lsm", [128, 2], F32, st)
        nlam = k.sb("nlam", [128, 1], F32, st)
        gsub = k.sb("gsub", [128, 1], F32, st)
        onesf = k.sb("onesf", [1, 128], F32, st)
        k.memset("dve", onesf[:], 1.0, [onesf])
        k.dma("sp", lrow[:], g.da_lambda[slot:slot + 1, :, :].rearrange("o a d -> o (a d)"), reads=[g.da_lambda], writes=[lrow])
        k.mm(PS[0][:, :256], onesf[:], lrow[:], True, True, [onesf, lrow], [PS[0]])
        k.cp("dve", lbc[:].rearrange("p a d -> p (a d)"), PS[0][:, :256], [PS[0]], [lbc])
        lv = lbc[:].rearrange("p (a two) d -> p a two d", two=2)
        k.tt("dve", lpr[:], lv[:, :, 0, :], lv[:, :, 1, :], ALU.mult, [lbc], [lpr])
        k.op("dve", lambda e: e.reduce_sum(lsm[:], lpr[:], axis=AX.X), [lpr], [lsm])
        k.act(lsm[:], lsm[:], AF.Exp, [lsm], [lsm])
        k.tt("dve", nlam[:], lsm[:, 1:2], lsm[:, 0:1], ALU.subtract, [lsm], [nlam])
        k.ts("dve", nlam[:], nlam[:], -float(lam_init), None, ALU.add, None, [nlam], [nlam])
        k.dma("sp", gsub[:], g.da_subln[slot].rearrange("(p o) -> p o", o=1), reads=[g.da_subln], writes=[gsub])
        k.ts("dve", gsub[:], gsub[:], float(1.0 - lam_init), None, ALU.mult, None, [gsub], [gsub])

        qT = k.sb("qT", [128, 8, TT], BF16, st)
        kT = k.sb("kT", [128, 8, TT], BF16, st)
        V = k.sb("Vt", [128, 18, D], BF16, st)
        uT = k.sb("uT", [128, 8, TT], BF16, st)
        for b in range(NBC):
            mixer_prenorm_all(g, b, uT, st)
            with Scope(k) as s2:
                wps = [k.sb(f"wp{i}", [128, 8, D], BF16, s2) for i in range(2)]
                rts = [k.sb(f"rt{i}", [128, 2, 512], F32, s2) for i in range(2)]
                t1s = [k.sb(f"t1{i}", [128, 512], F32, s2) for i in range(2)]
                t2s = [k.sb(f"t2{i}", [128, 512], F32, s2) for i in range(2)]
                n = 0
                for (dst, off) in ((qT, 0), (kT, D)):
                    w0, w1 = wps
                    for c in range(8):
                        k.dma("pool", w0[:, c, :], g.da_w_in[slot, c * 128:(c + 1) * 128, off:off + D], reads=[g.da_w_in], writes=[(w0, c)])
                        k.dma("pool", w1[:, c, :], g.da_w_sw[slot, c * 128:(c + 1) * 128, off:off + D], reads=[g.da_w_sw], writes=[(w1, c)])
                    for (t0, w) in [(0, 256)] + [(C + i * 512, 512) for i in range(4)]:
                        lat = t0 >= C
                        if lat:
                            rt = rts[n % 2]
                            k.dma("sp", rt[:], g.rope[:, :, t0 - C:t0 - C + 512].rearrange("a p t -> p a t"), reads=[g.rope], writes=[rt])
                        for h in range(8):
                            pq = PS[(h % 2) * 2]; pqs = PS[(h % 2) * 2 + 1]
                            for c in range(8):
                                k.mm(pq[:, :w], w0[:, c, h * 128:(h + 1) * 128], uT[:, c, t0:t0 + w], c == 0, c == 7, [(w0, c), uT], [pq])
                            if not lat:
                                k.cp("act", dst[:, h, t0:t0 + w], pq[:, :w], [pq], [(dst, h)])
                                continue
                            for c in range(8):
                                k.mm(pqs[:, :w], w1[:, c, h * 128:(h + 1) * 128], uT[:, c, t0:t0 + w], c == 0, c == 7, [(w1, c), uT], [pqs])
                            t1 = t1s[h % 2]; t2 = t2s[h % 2]
                            k.tt("dve", t1[:], pq[:], rt[:, 0, :], ALU.mult, [pq, rt], [t1])
                            k.tt("dve", t2[:], pqs[:], rt[:, 1, :], ALU.mult, [pqs, rt], [t2])
                            k.tt("dve", dst[:, h, t0:t0 + w], t1[:], t2[:], ALU.add, [t1, t2], [(dst, h)])
                        n += 1
                wv = wps[0]
                for c in range(8):
                    k.dma("pool", wv[:, c, :], g.da_w_in[slot, c * 128:(c + 1) * 128, 2 * D:3 * D], reads=[g.da_w_in], writes=[(wv, c)])
                for kc in range(18):
                    for half in range(2):
                        pv = PS[(kc % 2) * 2 + half]
                        for c in range(8):
                            k.mm(pv[:], uT[:, c, kc * 128:(kc + 1) * 128], wv[:, c, half * 512:(half + 1) * 512], c == 0, c == 7, [uT, (wv, c)], [pv])
                        k.cp("act" if half else "dve", V[:, kc, half * 512:(half + 1) * 512], pv[:], [pv], [(V, kc)])
            import os
            if os.environ.get("DA_DBG") and b == 0:
                k.dma("pool", g.dbg[1, 0:128, :], qT[:, 0, :], reads=[qT], writes=[g.dbg])
                k.dma("pool", g.dbg[1, 128:256, :], kT[:, 0, :], reads=[kT], writes=[g.dbg])
                k.dma("pool", g.dbg[1, 256:384, 0:1024], V[:, 3, :], reads=[V], writes=[g.dbg])
                k.dma("pool", g.dbg[1, 384:512, :], uT[:, 0, :], reads=[uT], writes=[g.dbg])
                g.skip_dbg_copy = True
                return
            aT = uT
            with Scope(k) as s2:
                E1s = [k.sb(f"E1{i}", [128, 512], BF16, s2) for i in range(2)]
                E2s = [k.sb(f"E2{i}", [128, 512], BF16, s2) for i in range(2)]
                fr1 = k.sb("fr1", [128, 512], F32, s2); fr2 = k.sb("fr2", [128, 512], F32, s2)
                fo = k.sb("fo", [128, 512], F32, s2); fo2 = k.sb("fo2", [128, 512], F32, s2)
                fsq = k.sb("fsq", [128, 512], BF16, s2)
                qtiles = ([] if last else [(0, 256, 2)]) + [(C + i * 512, 512, 18) for i in range(4)]
                units = [(h, t0, w, nkc) for h in range(8) for (t0, w, nkc) in qtiles]
                O1, Z1, O2, Z2 = PS[4], PS[5], PS[6], PS[7]

                def fin1(w):
                    k.rcp(fr1[:, :w], Z1[:, :w], [Z1], [fr1])
                    k.rcp(fr2[:, :w], Z2[:, :w], [Z2], [fr2])
                    k.tt("dve", fo[:, :w], O1[:, :w], fr1[:, :w], ALU.mult, [O1, fr1], [fo])
                    k.tt("dve", fo2[:, :w], O2[:, :w], fr2[:, :w], ALU.mult, [O2, fr2], [fo2])

                def fin2(h, t0, w, ss):
                    k.stt("dve", fo[:, :w], fo2[:, :w], nlam[:, 0:1], fo[:, :w], ALU.mult, ALU.add, [fo2, nlam, fo], [fo])
                    k.op("act", lambda e: e.square(fsq[:, :w], fo[:, :w]), [fo], [fsq])
                    k.mm(ss[:, :w], g.ones[:], fsq[:, :w], True, True, [g.ones, fsq], [ss])
                    k.act(fr1[:, :w], ss[:, :w], AF.Sqrt, [ss], [fr1], bias=EPS, scale=1.0 / 128)
                    k.rcp(fr1[:, :w], fr1[:, :w], [fr1], [fr1])
                    k.tt("dve", fo[:, :w], fo[:, :w], fr1[:, :w], ALU.mult, [fo, fr1], [fo])
                    k.ts("dve", aT[:, h, t0:t0 + w], fo[:, :w], gsub[:, 0:1], None, ALU.mult, None, [fo, gsub], [(aT, h)])

                pend = None
                for (h, t0, w, nkc) in units:
                    def qk(kc):
                        s1 = PS[(kc % 2) * 2]; s2p = PS[(kc % 2) * 2 + 1]
                        k.mm(s1[:, :w], kT[0:64, h, kc * 128:(kc + 1) * 128], qT[0:64, h, t0:t0 + w], True, True, [(kT, h), (qT, h)], [s1])
                        k.mm(s2p[:, :w], kT[64:128, h, kc * 128:(kc + 1) * 128], qT[64:128, h, t0:t0 + w], True, True, [(kT, h), (qT, h)], [s2p])
                    qk(0)
                    for kc in range(nkc):
                        s1 = PS[(kc % 2) * 2]; s2p = PS[(kc % 2) * 2 + 1]
                        E1 = E1s[kc % 2]; E2 = E2s[kc % 2]
                        k.act(E1[:, :w], s1[:, :w], AF.Exp, [s1], [E1], scale=0.125)
                        k.act(E2[:, :w], s2p[:, :w], AF.Exp, [s2p], [E2], scale=0.125)
                        if kc + 1 < nkc:
                            qk(kc + 1)
                        fst = kc == 0; lst = kc == nkc - 1
                        k.mm(O1[:, :w], V[:, kc, h * 128:(h + 1) * 128], E1[:, :w], fst, lst, [(V, kc), E1], [O1])
                        k.mm(Z1[:, :w], g.ones[:], E1[:, :w], fst, lst, [g.ones, E1], [Z1])
                        k.mm(O2[:, :w], V[:, kc, h * 128:(h + 1) * 128], E2[:, :w], fst, lst, [(V, kc), E2], [O2])
                        k.mm(Z2[:, :w], g.ones[:], E2[:, :w], fst, lst, [g.ones, E2], [Z2])
                        if pend is not None and kc == min(1, nkc - 1):
                            fin2(*pend, PS[(kc % 2) * 2 + 1])
                            pend = None
                    fin1(w)
                    pend = (h, t0, w)
                fin2(*pend, PS[1])
            if os.environ.get("DA_DBG3") and b == 0:
                for h in range(8):
                    k.dma("pool", g.dbg[1, h * 128:(h + 1) * 128, :], aT[:, h, :], reads=[aT], writes=[g.dbg])
                g.skip_dbg_copy = True
            mixer_outproj(g, b, aT, g.da_w_out[slot], g.da_w_out, last)


def na_chunks(r):
    rs = min(max(r - 4, 0), 24)
    out = []
    for m in range(rs // 2, (rs + 7) // 2 + 1):
        if 4 <= r <= 28:
            boff = (2 * m - r + 5) * 64
        elif r < 4:
            boff = 640 + r * 512 + (2 * m) * 64
        else:
            boff = 640 + (4 + r - 28) * 512 + (2 * m - 24) * 64
        out.append((2 + m, boff))
    return out


def na_phase(g, l):
    import os
    k = g.k
    PS = g.PS
    set_mod(g, l, 3, 4, 5, 2, 3, 1.0)
    with Scope(k) as st:
        qT = k.sb("nqT", [128, 8, TT], BF16, st)
        kT = k.sb("nkT", [128, 8, TT], BF16, st)
        Vt = k.sb("nVt", [128, 18, 16 * 65], BF16, st)
        uT = k.sb("nuT", [128, 8, TT], BF16, st)
        k.memset("pool", Vt[:], 1.0, [Vt])
        for b in range(NBC):
            mixer_prenorm_all(g, b, uT, st)
            with Scope(k) as s2:
                wps = [k.sb(f"nwp{i}", [128, 8, D], BF16, s2) for i in range(2)]
                for pi, (dst, off) in enumerate(((qT, 0), (kT, D))):
                    w0 = wps[pi]
                    for c in range(8):
                        k.dma("pool", w0[:, c, :], g.na_w_in[0, c * 128:(c + 1) * 128, off:off + D], reads=[g.na_w_in], writes=[(w0, c)])
                    for (t0, w) in [(0, 256)] + [(C + i * 512, 512) for i in range(4)]:
                        for h in range(8):
                            pq = PS[h % 4]
                            for c in range(8):
                                k.mm(pq[:, :w], w0[:, c, h * 128:(h + 1) * 128], uT[:, c, t0:t0 + w], c == 0, c == 7, [(w0, c), uT], [pq])
                            if pi == 0:
                                k.op("act", lambda e, o=dst[:, h, t0:t0 + w], i=pq[:, :w]: e.mul(o, i, 0.125), [pq], [(dst, h)])
                            else:
                                k.cp("dve", dst[:, h, t0:t0 + w], pq[:, :w], [pq], [(dst, h)])
                wv = wps[0]
                for c in range(8):
                    k.dma("pool", wv[:, c, :], g.na_w_in[0, c * 128:(c + 1) * 128, 2 * D:3 * D], reads=[g.na_w_in], writes=[(wv, c)])
                for kc in range(18):
                    for half in range(2):
                        pv = PS[(kc % 2) * 2 + half]
                        for c in range(8):
                            k.mm(pv[:], uT[:, c, kc * 128:(kc + 1) * 128], wv[:, c, half * 512:(half + 1) * 512], c == 0, c == 7, [uT, (wv, c)], [pv])
                        dest = Vt[:, kc, half * 520:(half + 1) * 520].rearrange("p (h e) -> p h e", e=65)[:, :, 0:64]
                        k.cp("act" if half else "dve", dest, pv[:].rearrange("p (h e) -> p h e", e=64), [pv], [(Vt, kc)])
            aT = uT
            with Scope(k) as s2:
                bts = [k.sb(f"nbt{i}", [128, NA_BW], BF16, s2) for i in range(2)]
                Es = [k.sb(f"nE{i}", [128, 512], BF16, s2) for i in range(4)]
                rzs = [k.sb(f"nrz{i}", [128, 2], F32, s2) for i in range(2)]
                ons = [k.sb(f"non{i}", [128, 128], F32, s2) for i in range(2)]

                def fin_a(po, nq, par):
                    rz, on = rzs[par], ons[par]
                    pov = po[:nq, 0:130].rearrange("p (h e) -> p h e", e=65)
                    k.rcp(rz[:nq, :], pov[:, :, 64], [po], [rz])
                    k.tt("dve", on[:nq, :].rearrange("p (h e) -> p h e", e=64), pov[:, :, 0:64],
                         rz[:nq, :].unsqueeze(2).to_broadcast([nq, 2, 64]), ALU.mult, [po, rz], [on])

                def fin_b(nq, c, tok0, pt, par):
                    on = ons[par]
                    k.tr(pt[:, :nq], on[:nq, :], g.ident[:nq, :nq], [on, g.ident], [pt])
                    k.cp("act", aT[:, c, tok0:tok0 + nq], pt[:, :nq], [pt], [(aT, c)])

                for c in range(8):
                    bt = bts[c % 2]
                    k.dma("pool", bt[0:64, :], g.na_bias[:, 2 * c, :], reads=[g.na_bias], writes=[bt])
                    k.dma("pool", bt[64:128, :], g.na_bias[:, 2 * c + 1, :], reads=[g.na_bias], writes=[bt])
                    for qh in range(2):
                        po = PS[4 + qh % 2]
                        for hh in range(2):
                            h = 2 * c + hh
                            pb = hh * 64
                            sp_ = PS[hh]
                            for j in range(2):
                                k.mm(sp_[:, j * 128:(j + 1) * 128], kT[pb:pb + 64, c, j * 128:(j + 1) * 128],
                                     qT[pb:pb + 64, c, qh * 128:(qh + 1) * 128], True, True, [(kT, c), (qT, c)], [sp_])
                            E = Es[hh]
                            k.act(E[:, :256], sp_[:, :256], AF.Exp, [sp_], [E])
                            for j in range(2):
                                k.mm(po[:, hh * 65:(hh + 1) * 65], E[:, j * 128:(j + 1) * 128], Vt[:, j, h * 65:(h + 1) * 65],
                                     j == 0, j == 1, [E, (Vt, j)], [po])
                        fin_a(po, 128, qh % 2)
                        fin_b(128, c, qh * 128, PS[6 + qh % 2], qh % 2)

                    def st_scores(r):
                        tq0 = C + r * 64
                        chunks = [(0, None), (1, None)] + na_chunks(r)
                        nch = len(chunks)
                        for hh in range(2):
                            pb = hh * 64
                            sp_ = PS[(r % 2) * 2 + hh]
                            for j, (kc, boff) in enumerate(chunks):
                                k.mm(sp_[:, j * 64:(j + 1) * 64], kT[pb:pb + 64, c, kc * 128:(kc + 1) * 128],
                                     qT[pb:pb + 64, c, tq0:tq0 + 64], True, boff is None, [(kT, c), (qT, c)], [sp_])
                                if boff is not None:
                                    k.mm(sp_[:, j * 64:(j + 1) * 64], bt[pb:pb + 64, boff:boff + 128], g.identb[pb:pb + 64, pb:pb + 64],
                                         False, True, [bt, g.identb], [sp_])
                        for hh in range(2):
                            sp_ = PS[(r % 2) * 2 + hh]
                            E = Es[(r % 2) * 2 + hh]
                            k.act(E[:, :nch * 64], sp_[:, :nch * 64], AF.Exp, [sp_], [E])

                    def st_pv(r):
                        chunks = [(0, None), (1, None)] + na_chunks(r)
                        nch = len(chunks)
                        po = PS[4 + r % 2]
                        for hh in range(2):
                            h = 2 * c + hh
                            E = Es[(r % 2) * 2 + hh]
                            for j, (kc, boff) in enumerate(chunks):
                                k.mm(po[0:64, hh * 65:(hh + 1) * 65], E[:, j * 64:(j + 1) * 64], Vt[:, kc, h * 65:(h + 1) * 65],
                                     j == 0, j == nch - 1, [E, (Vt, kc)], [po])
                        fin_a(po, 64, r % 2)

                    st_scores(0)
                    for r in range(32):
                        if r + 1 < 32:
                            st_scores(r + 1)
                        st_pv(r)
                        if r >= 1:
                            fin_b(64, c, C + (r - 1) * 64, PS[6 + (r - 1) % 2], (r - 1) % 2)
                    fin_b(64, c, C + 31 * 64, PS[6 + 31 % 2], 31 % 2)
            if os.environ.get("NA_DBG") and b == 0:
                for h in range(8):
                    k.dma("pool", g.dbg[1, h * 128:(h + 1) * 128, :], aT[:, h, :], reads=[aT], writes=[g.dbg])
                k.dma("pool", g.dbg[0, 0:128, :], qT[:, 0, :], reads=[qT], writes=[g.dbg])
                k.dma("pool", g.dbg[0, 128:256, :], kT[:, 0, :], reads=[kT], writes=[g.dbg])
                k.dma("pool", g.dbg[0, 256:384, 0:1040], Vt[:, 1, :], reads=[Vt], writes=[g.dbg])
                g.skip_dbg_copy = True
            mixer_outproj(g, b, aT, g.na_w_out[0], g.na_w_out, False)


RW_C0 = 0.6065306597126334
NCH = 36


def rw_phase(g, l):
    import os
    k = g.k
    PS = g.PS
    nc = k.nc
    set_mod(g, l, 3, 4, 5, 2, 3, 1.0)
    if not hasattr(g, "rw_scr"):
        d = {}
        d["NT"] = k.dram("rw_NT", [2, 2 * NCH * 16, 4096], F32)
        d["TTs"] = k.dram("rw_TT", [2, 2 * NCH * 16, 4096], BF16)
        for nm in ("ARB", "AAK", "ARK"):
            d[nm] = k.dram("rw_" + nm, [2, NBC, NCH, 16, 64, 64], BF16)
        d["TM"] = k.dram("rw_TM", [2, NBC, 3, TT, D], BF16)
        d["V"] = k.dram("rw_V", [NBC, TT, D], BF16)
        d["RT"] = k.dram("rw_RT", [2, NBC, D, TT], BF16)
        d["WT"] = k.dram("rw_WT", [2, NBC, D, NCH], F32)
        d["Y"] = k.dram("rw_Y", [2, NBC, TT, D], F32)
        d["BVG"] = k.dram("rw_BVG", [NBC, 2, TT, D], F32)
        g.rw_scr = d
    d = g.rw_scr
    A, ACT_, DVE, POOL = ALU, "act", "dve", "pool"

    with Scope(k) as st:
        cst = k.sb("rwc", [128, 5, 128], F32, st)
        k.dma("sp", cst[:], g.rwconst[:].rearrange("a p t -> p a t"), reads=[g.rwconst], writes=[cst])
        onesf = k.sb("ronesf", [1, 128], F32, st)
        k.memset(DVE, onesf[:], 1.0, [onesf])
        prow = k.sb("prow", [1, D], F32, st)
        bc = {}
        for nm, row in (("w00", V_W0), ("w01", V_W0 + 1), ("a00", V_A0), ("a01", V_A0 + 1), ("kk", V_KK), ("ka", V_KA), ("rk", V_RK)):
            t = k.sb("bc_" + nm, [128, D], F32, st)
            k.dma("sp", prow[:], g.vecs[row:row + 1, :], reads=[g.vecs], writes=[prow])
            for half in range(2):
                k.mm(PS[half][:], onesf[:], prow[:, half * 512:(half + 1) * 512], True, True, [onesf, prow], [PS[half]])
                k.cp(ACT_ if half else DVE, t[:, half * 512:(half + 1) * 512], PS[half][:], [PS[half]], [t])
            bc[nm] = t
        omka = k.sb("bc_omka", [128, D], F32, st)
        k.ts(DVE, omka[:], bc["ka"][:], -1.0, 1.0, A.mult, A.add, [bc["ka"]], [omka])
        win = k.sb("rwin", [128, 8, 3 * D], BF16, st)
        w1 = k.sb("rw1", [128, 8, 2, 64], BF16, st)
        a1 = k.sb("ra1", [128, 8, 2, 64], BF16, st)
        g1 = k.sb("rg1", [128, 8, 160], BF16, st)
        w2 = k.sb("rw2", [64, 2, D], BF16, st)
        a2 = k.sb("ra2", [64, 2, D], BF16, st)
        g2 = k.sb("rg2", [128, 2, D], BF16, st)
        for c in range(8):
            k.dma(POOL, win[:, c, :], g.rw_w_in[0, c * 128:(c + 1) * 128, :], reads=[g.rw_w_in], writes=[(win, c)])
        for z in range(2):
            k.dma(POOL, w1[:, :, z, :], g.rw_w1[0, z].rearrange("(c p) r -> p c r", p=128), reads=[g.rw_w1], writes=[w1])
            k.dma(POOL, a1[:, :, z, :], g.rw_a1[0, z].rearrange("(c p) r -> p c r", p=128), reads=[g.rw_a1], writes=[a1])
            k.dma(POOL, w2[:, z, :], g.rw_w2[0, z], reads=[g.rw_w2], writes=[w2])
            k.dma(POOL, a2[:, z, :], g.rw_a2[0, z], reads=[g.rw_a2], writes=[a2])
        k.dma(POOL, g1[:], g.rw_g1[0].rearrange("(c p) r -> p c r", p=128), reads=[g.rw_g1], writes=[g1])
        k.dma(POOL, g2[:, 0, :], g.rw_g2[0, 0:128, :], reads=[g.rw_g2], writes=[g2])
        k.dma(POOL, g2[0:32, 1, :], g.rw_g2[0, 128:160, :], reads=[g.rw_g2], writes=[g2])
        xT = k.sb("rxT", [128, 8, 130], F32, st)
        U = k.sb("rU", [128, 8, 130], F32, st)
        sq = k.sb("rsq", [128, 8, 130], BF16, st)
        tmp = k.sb("rtmp", [128, 8, 130], F32, st)
        rstd = k.sb("rrstd", [128, 130], F32, st)
        xj = [k.sb(f"rxj{j}", [128, 8, 128], BF16, st) for j in range(6)]
        hT = [k.sb(f"rhT{i}", [128, 128], BF16, st) for i in range(6)]
        Tt = [k.sb(f"rT{i}", [128, D], F32, st) for i in range(12)]
        FM = k.sb("rFM", [128, 8, 2, 4, 64], BF16, st)
        MMt = k.sb("rMM", [128, 16, 128], F32, st)
        WTt = k.sb("rWT", [128, 8, 2], F32, st)
        hsum = k.sb("rhs", [128, 16], F32, st)
        (Tr, Tk, Tv, Tkk, Tsg, Ta, Tkd, Tb, Tcs, Tx, Ty, Tks) = Tt
        for b in range(NBC):
            col_lat = b
            for ti in range(18):
                t0 = ti * 128
                col = 2 if t0 < C else col_lat
                seq0, seq1 = (0, C) if t0 < C else (C, TT)
                lo = max(t0 - 1, seq0)
                hi = min(t0 + 129, seq1)
                o0 = lo - (t0 - 1)
                nl = hi - lo
                k.dma("sp", xT[:, :, o0:o0 + nl], g.hT[b, :, lo:hi].rearrange("(c p) t -> p c t", p=128),
                      reads=hslots(g, b, lo, nl), writes=[xT])
                if o0 > 0:
                    k.memset(POOL, xT[:, :, 0:1], 1.0, [xT])
                if o0 + nl < 130:
                    k.memset(POOL, xT[:, :, 129:130], 1.0, [xT])
                prenorm(g, xT, 130, col, U, sq, tmp, rstd, PS[6])
                if o0 > 0:
                    k.memset(POOL, U[:, :, 0:1], 0.0, [U])
                if o0 + nl < 130:
                    k.memset(POOL, U[:, :, 129:130], 0.0, [U])
                xxv = xT[:, :, 0:128]
                k.tt(DVE, xxv, U[:, :, 0:128], U[:, :, 2:130], A.add, [U], [xT])
                k.stt(DVE, xxv, xxv, 0.5, U[:, :, 1:129], A.mult, A.subtract, [xT, U], [xT])
                for j in range(6):
                    mu = g.vp[:, V_MU + j, :].unsqueeze(2).to_broadcast([128, 8, 128])
                    k.tt(DVE, tmp[:, :, 0:128], xxv, mu, A.mult, [xT, g.vp], [tmp])
                    k.tt(DVE, xj[j][:], tmp[:, :, 0:128], U[:, :, 1:129], A.add, [tmp, U], [xj[j]])
                xr, xw, xk, xv, xa, xg = xj
                for (xs_, off, dst) in ((xr, 0, Tr), (xk, D, Tk), (xv, 2 * D, Tv)):
                    for half in range(2):
                        p = PS[half]
                        for c in range(8):
                            k.mm(p[:], xs_[:, c, :], win[:, c, off + half * 512: off + (half + 1) * 512], c == 0, c == 7, [xs_, (win, c)], [p])
                        k.cp(ACT_ if half else DVE, dst[:, half * 512:(half + 1) * 512], p[:], [p], [dst])
                k.dma(POOL, d["V"][b, t0:t0 + 128, :], Tv[:], reads=[Tv], writes=[d["V"]])
                for z in range(2):
                    p = PS[2]
                    for c in range(8):
                        k.mm(p[0:64, 0:128], w1[:, c, z, :], xw[:, c, :], c == 0, c == 7, [w1, xw], [p])
                    k.act(hT[z][0:64, :], p[0:64, 0:128], AF.Tanh, [p], [hT[z]])
                    p = PS[3]
                    for c in range(8):
                        k.mm(p[0:64, 0:128], a1[:, c, z, :], xa[:, c, :], c == 0, c == 7, [a1, xa], [p])
                    k.cp(DVE, hT[2 + z][0:64, :], p[0:64, 0:128], [p], [hT[2 + z]])
                p = PS[2]
                for c in range(8):
                    k.mm(p[:, 0:128], g1[:, c, 0:128], xg[:, c, :], c == 0, c == 7, [g1, xg], [p])
                k.act(hT[4][:, :], p[:, 0:128], AF.Sigmoid, [p], [hT[4]])
                p = PS[3]
                for c in range(8):
                    k.mm(p[0:32, 0:128], g1[:, c, 128:160], xg[:, c, :], c == 0, c == 7, [g1, xg], [p])
                k.act(hT[5][0:32, :], p[0:32, 0:128], AF.Sigmoid, [p], [hT[5]])
                for half in range(2):
                    p = PS[half]
                    k.mm(p[:], hT[4][:, :], g2[:, 0, half * 512:(half + 1) * 512], True, False, [hT[4], g2], [p])
                    k.mm(p[:], hT[5][0:32, :], g2[0:32, 1, half * 512:(half + 1) * 512], False, True, [hT[5], g2], [p])
                    k.cp(ACT_ if half else DVE, Tx[:, half * 512:(half + 1) * 512], p[:], [p], [Tx])
                k.dma("sp", d["BVG"][b, 1, t0:t0 + 128, :], Tx[:], reads=[Tx], writes=[d["BVG"]])
                k.tt(DVE, Tkk[:], Tk[:], bc["kk"][:], A.mult, [Tk, bc["kk"]], [Tkk])
                k.op(ACT_, lambda e: e.square(Ty[:], Tkk[:]), [Tkk], [Ty])
                k.op(DVE, lambda e: e.reduce_sum(hsum[:], Ty[:].rearrange("p (h e) -> p h e", e=64), axis=AX.X), [Ty], [hsum])
                k.act(hsum[:], hsum[:], AF.Sqrt, [hsum], [hsum])
                k.ts(DVE, hsum[:], hsum[:], 1e-12, None, A.max, None, [hsum], [hsum])
                k.rcp(hsum[:], hsum[:], [hsum], [hsum])
                k.tt(DVE, Tkk[:].rearrange("p (h e) -> p h e", e=64), Tkk[:].rearrange("p (h e) -> p h e", e=64),
                     hsum[:].unsqueeze(2).to_broadcast([128, 16, 64]), A.mult, [Tkk, hsum], [Tkk])
                for z in range(2):
                    zb = z * NBC + b
                    for half in range(2):
                        p = PS[half]
                        k.mm(p[:], hT[z][0:64, :], w2[:, z, half * 512:(half + 1) * 512], True, True, [hT[z], w2], [p])
                        k.tt(DVE, Tsg[:, half * 512:(half + 1) * 512], p[:], bc["w0%d" % z][:, half * 512:(half + 1) * 512], A.add, [p, bc["w0%d" % z]], [Tsg])
                        p = PS[2 + half]
                        k.mm(p[:], hT[2 + z][0:64, :], a2[:, z, half * 512:(half + 1) * 512], True, True, [hT[2 + z], a2], [p])
                        k.tt(DVE, Ta[:, half * 512:(half + 1) * 512], p[:], bc["a0%d" % z][:, half * 512:(half + 1) * 512], A.add, [p, bc["a0%d" % z]], [Ta])
                    k.act(Tsg[:], Tsg[:], AF.Sigmoid, [Tsg], [Tsg])
                    k.act(Ta[:], Ta[:], AF.Sigmoid, [Ta], [Ta])
                    k.tt(DVE, Tkd[:], Ta[:], bc["ka"][:], A.mult, [Ta, bc["ka"]], [Tkd])
                    k.tt(DVE, Tkd[:], Tkd[:], omka[:], A.add, [Tkd, omka], [Tkd])
                    k.tt(DVE, Tkd[:], Tkd[:], Tk[:], A.mult, [Tkd, Tk], [Tkd])
                    if z == 0:
                        k.cp(ACT_, Tks[:], Tkd[:], [Tkd], [Tks])
                    else:
                        k.tt(DVE, Tks[:], Tks[:], Tkd[:], A.add, [Tks, Tkd], [Tks])
                    k.tt(DVE, Tb[:], Tkk[:], Ta[:], A.mult, [Tkk, Ta], [Tb])
                    csP = (PS[0], PS[1]); totP = (PS[2], PS[3])
                    for half in range(2):
                        k.mm(csP[half][:], cst[:, 2 + z, :], Tsg[:, half * 512:(half + 1) * 512], True, True, [cst, Tsg], [csP[half]])
                        k.mm(totP[half][:], cst[:, 4, :], Tsg[:, half * 512:(half + 1) * 512], True, True, [cst, Tsg], [totP[half]])
                    for half in range(2):
                        hs = slice(half * 512, (half + 1) * 512)
                        k.cp(ACT_, Tcs[:, hs], csP[half][:], [csP[half]], [Tcs])
                        k.tt(DVE, Tx[:, hs], Tcs[:, hs], Tsg[:, hs], A.subtract, [Tcs, Tsg], [Tx])
                        k.act(Tx[:, hs], Tx[:, hs], AF.Exp, [Tx], [Tx], scale=-RW_C0)
                        k.tt(DVE, Ty[:, hs], totP[half][:], Tcs[:, hs], A.subtract, [totP[half], Tcs], [Ty])
                        k.act(Ty[:, hs], Ty[:, hs], AF.Exp, [Ty], [Ty], scale=-RW_C0)
                    k.stt(DVE, Tx[:], Tkk[:], -1.0, Tx[:], A.mult, A.mult, [Tkk, Tx], [Tx])
                    k.dma(POOL, d["TM"][z, b, 0, t0:t0 + 128, :], Tx[:], reads=[Tx], writes=[d["TM"]])
                    BWt = Tsg
                    k.tt(DVE, BWt[:], Tb[:], Ty[:], A.mult, [Tb, Ty], [BWt])
                    k.dma(POOL, d["TM"][z, b, 1, t0:t0 + 128, :], BWt[:], reads=[BWt], writes=[d["TM"]])
                    k.tt(DVE, Ty[:], Tkd[:], Ty[:], A.mult, [Tkd, Ty], [Ty])
                    k.dma(POOL, d["TM"][z, b, 2, t0:t0 + 128, :], Ty[:], reads=[Ty], writes=[d["TM"]])
                    Trt, Tbt, Tkt, TW = Ta, Tb, Tkd, Tcs
                    k.act(Ta[:], Tcs[:], AF.Exp, [Tcs], [Ta], scale=-RW_C0)
                    k.tt(DVE, Trt[:], Ta[:], Tr[:], A.mult, [Ta, Tr], [Trt])
                    k.act(BWt[:], Tcs[:], AF.Exp, [Tcs], [BWt], scale=RW_C0)
                    k.tt(DVE, Tbt[:], Tb[:], BWt[:], A.mult, [Tb, BWt], [Tbt])
                    k.tt(DVE, Tkt[:], Tkd[:], BWt[:], A.mult, [Tkd, BWt], [Tkt])
                    for half in range(2):
                        k.act(TW[:, half * 512:(half + 1) * 512], totP[half][:], AF.Exp, [totP[half]], [TW], scale=-RW_C0)
                    n = 0
                    for c in range(8):
                        for qi, src in enumerate((Tx, Trt, Tbt, Tkt)):
                            p = PS[4 + n % 4]
                            k.tr(p[:, 0:128], src[:, c * 128:(c + 1) * 128], g.ident[:], [src, g.ident], [p])
                            k.cp(ACT_ if n % 2 else DVE, FM[:, c, :, qi, :], p[:, 0:128].rearrange("p (a t) -> p a t", a=2), [p], [FM])
                            n += 1
                        p = PS[4 + n % 4]
                        k.tr(p[:, 0:128], TW[:, c * 128:(c + 1) * 128], g.ident[:], [TW, g.ident], [p])
                        k.cp(DVE, WTt[:, c, :], p[:, 0:128].rearrange("p (a t) -> p a t", a=2)[:, :, 0], [p], [WTt])
                        n += 1
                    k.dma("sp", d["WT"][z, b, :, 2 * ti:2 * ti + 2].rearrange("(c p) a -> p c a", p=128), WTt[:], reads=[WTt], writes=[d["WT"]])
                    for ch in range(2):
                        k.dma("act", d["RT"][z, b, :, t0 + ch * 64:t0 + (ch + 1) * 64].rearrange("(c p) t -> p c t", p=128),
                              FM[:, :, ch, 1, :], reads=[FM], writes=[d["RT"]])
                    n = 0
                    for ch in range(2):
                        for c in range(8):
                            for hh in range(2):
                                pb = hh * 64
                                p = PS[n % 4]
                                k.mm(p[:, 0:128], FM[pb:pb + 64, c, ch, 2:4, :].rearrange("p a t -> p (a t)"),
                                     FM[pb:pb + 64, c, ch, 0:2, :].rearrange("p a t -> p (a t)"), True, True, [FM], [p])
                                k.tt(DVE, MMt[:, 2 * c + hh, :], p[:, 0:128], cst[:, z, :], A.mult, [p, cst], [MMt])
                                n += 1
                        chunk = 2 * ti + ch
                        u0 = (b * NCH + chunk) * 16
                        k.dma("sp", d["NT"][z, u0:u0 + 16, :].rearrange("h (s t) -> s h t", t=64), MMt[0:64, :, 0:64], reads=[MMt], writes=[d["NT"]])
                        k.dma(POOL, d["ARB"][z, b, chunk].rearrange("h s t -> s h t"), MMt[0:64, :, 64:128], reads=[MMt], writes=[d["ARB"]])
                        k.dma(POOL, d["AAK"][z, b, chunk].rearrange("h s t -> s h t"), MMt[64:128, :, 0:64], reads=[MMt], writes=[d["AAK"]])
                        k.dma(POOL, d["ARK"][z, b, chunk].rearrange("h s t -> s h t"), MMt[64:128, :, 64:128], reads=[MMt], writes=[d["ARK"]])
                k.tt(DVE, Tx[:], Tr[:], Tks[:], A.mult, [Tr, Tks], [Tx])
                k.tt(DVE, Tx[:], Tx[:], bc["rk"][:], A.mult, [Tx, bc["rk"]], [Tx])
                k.op(DVE, lambda e: e.reduce_sum(hsum[:], Tx[:].rearrange("p (h e) -> p h e", e=64), axis=AX.X), [Tx], [hsum])
                k.tt(DVE, Tx[:].rearrange("p (h e) -> p h e", e=64), Tv[:].rearrange("p (h e) -> p h e", e=64),
                     hsum[:].unsqueeze(2).to_broadcast([128, 16, 64]), A.mult, [Tv, hsum], [Tx])
                k.dma("sp", d["BVG"][b, 0, t0:t0 + 128, :], Tx[:], reads=[Tx], writes=[d["BVG"]])
    stop = os.environ.get("RW_STOP", "")
    if stop == "A":
        return
    rw_solve(g)
    if stop == "B":
        return
    rw_scan(g)
    if stop == "C":
        return
    rw_readout(g)


def _diag(ap2d):
    return [ap2d[:, 0:4095].rearrange("p (a b) -> p a b", b=65)[:, :, 0], ap2d[:, 4095:4096]]


def rw_solve(g):
    k = g.k
    d = g.rw_scr
    with Scope(k) as st:
        NS = 3
        Ms = [k.sb(f"sM{i}", [128, 64, 64], F32, st) for i in range(NS)]
        Xs = [k.sb(f"sX{i}", [128, 64, 64], F32, st) for i in range(NS)]
        tps = [k.sb(f"sT{i}", [128, 64, 64], F32, st) for i in range(NS)]
        for z in range(2):
            for trip in range(3):
                grps = [trip * 3 + i for i in range(NS)]
                for i, grp in enumerate(grps):
                    M, X = Ms[i], Xs[i]
                    Mf = M[:].rearrange("p s t -> p (s t)")
                    Xf = X[:].rearrange("p s t -> p (s t)")
                    k.dma("sp", Mf, d["NT"][z, grp * 128:(grp + 1) * 128, :], reads=[d["NT"]], writes=[M])
                    for v in _diag(Mf):
                        k.memset("pool", v, 1.0, [M])
                    k.memset("pool", Xf, 0.0, [X])
                    for v in _diag(Xf):
                        k.memset("pool", v, 1.0, [X])
                rows = range(62, -1, -1) if z == 0 else range(1, 64)
                for s_ in rows:
                    lo, hi = (s_, 64) if z == 0 else (0, s_ + 1)
                    nt = hi - lo
                    for i in range(NS):
                        M, X, tp = Ms[i], Xs[i], tps[i]
                        k.tt("dve" if (i == 2 and s_ % 2 == 0) else "pool", tp[:, 0:nt, 0:nt], X[:, lo:hi, lo:hi], M[:, s_, lo:hi].unsqueeze(2).to_broadcast([128, nt, nt]), ALU.mult, [X, M], [tp])
                        k.op("dve", lambda e, o=X[:, s_, lo:hi], i_=tp[:, 0:nt, 0:nt].rearrange("p t j -> p j t"): e.reduce_sum(o, i_, axis=AX.X), [tp], [X])
                for i, grp in enumerate(grps):
                    Xf = Xs[i][:].rearrange("p s t -> p (s t)")
                    k.dma("pool", d["TTs"][z, grp * 128:(grp + 1) * 128, :], Xf, reads=[Xs[i]], writes=[d["TTs"]])


def rw_scan(g):
    k = g.k
    PS = g.PS
    d = g.rw_scr

    def v3(p0, p1):
        return [p0[0:64, :].rearrange("p (h e) -> p h e", e=64), p1[0:64, :].rearrange("p (h e) -> p h e", e=64)]

    def stream(z, b, st, tag):
        def two(nm, shape, dt):
            return [k.sb(f"{nm}{tag}{i}", shape, dt, st) for i in range(2)]
        Tts = two("cTt", [64, 16, 64], BF16); ARBs = two("cARB", [64, 16, 64], BF16)
        AAKs = two("cAAK", [64, 16, 64], BF16); ARKs = two("cARK", [64, 16, 64], BF16)
        TMs = two("cTM", [64, 3, D], BF16); Vs = two("cV", [64, D], BF16); RTs = two("cRT", [64, 16, 64], BF16)
        Wc = k.sb("cW" + tag, [64, 16, NCH], F32, st)
        X0s = k.sb("cX0" + tag, [64, 16, 64], BF16, st); Ah = k.sb("cAh" + tag, [64, 16, 64], BF16, st)
        U0s = k.sb("cU0" + tag, [64, 16, 64], BF16, st); GTs = k.sb("cGT" + tag, [64, 16, 64], BF16, st)
        QTs = k.sb("cQT" + tag, [64, 16, 64], BF16, st); DWt = k.sb("cDW" + tag, [64, 16, 64], F32, st)
        Hb = k.sb("cHb" + tag, [64, 16, 64], BF16, st)
        ybs = two("cyb", [64, D], F32)
        qa, qb = ("sp", "act") if z == 0 else ("act", "sp")
        k.dma(qa, Wc[:], d["WT"][z, b].rearrange("(h q) c -> q h c", q=64), reads=[d["WT"]], writes=[Wc])
        k.memset("dve", Hb[:], 0.0, [Hb])
        order = list(range(NCH)) if z == 0 else [3, 2, 1, 0] + list(range(NCH - 1, 3, -1))

        def load(ci, i):
            ch = order[ci]
            u0 = (b * NCH + ch) * 16
            k.dma(qa, Tts[i][:], d["TTs"][z, u0:u0 + 16, :].rearrange("h (s t) -> s h t", t=64), reads=[d["TTs"]], writes=[Tts[i]])
            k.dma(qb, ARBs[i][:], d["ARB"][z, b, ch].rearrange("h s t -> s h t"), reads=[d["ARB"]], writes=[ARBs[i]])
            k.dma(qa, AAKs[i][:], d["AAK"][z, b, ch].rearrange("h s t -> s h t"), reads=[d["AAK"]], writes=[AAKs[i]])
            k.dma(qb, ARKs[i][:], d["ARK"][z, b, ch].rearrange("h s t -> s h t"), reads=[d["ARK"]], writes=[ARKs[i]])
            k.dma(qa, TMs[i][:], d["TM"][z, b, :, ch * 64:(ch + 1) * 64, :].rearrange("q t f -> t q f"), reads=[d["TM"]], writes=[TMs[i]])
            k.dma(qb, Vs[i][:], d["V"][b, ch * 64:(ch + 1) * 64, :], reads=[d["V"]], writes=[Vs[i]])
            k.dma(qa, RTs[i][:], d["RT"][z, b, :, ch * 64:(ch + 1) * 64].rearrange("(h q) t -> q h t", q=64), reads=[d["RT"]], writes=[RTs[i]])

        load(0, 0)
        hs = lambda h: slice(h * 64, (h + 1) * 64)
        pc = lambda P2, h: P2[h // 8][0:64, (h % 8) * 64:(h % 8 + 1) * 64]
        pz = 0 if z == 0 else 4
        for ci in range(NCH):
            i = ci % 2
            ch = order[ci]
            if ci + 1 < NCH:
                load(ci + 1, (ci + 1) % 2)
            Tt_, ARB_, AAK_, ARK_, TMc, Vc, RTc = Tts[i], ARBs[i], AAKs[i], ARKs[i], TMs[i], Vs[i], RTs[i]
            PX = (PS[(pz + 0) % 8], PS[(pz + 1) % 8])
            for h in range(16):
                k.mm(pc(PX, h), AAK_[:, h, :], Vc[:, hs(h)], True, True, [AAK_, Vc], [PX[h // 8]])
            for q, pv in enumerate(v3(*PX)):
                k.cp("act" if q else "dve", X0s[:, q * 8:(q + 1) * 8, :], pv, [PX[q]], [X0s])
            yield
            PA = (PS[(pz + 2) % 8], PS[(pz + 3) % 8]); PU = (PS[(pz + 4) % 8], PS[(pz + 5) % 8])
            for h in range(16):
                k.mm(pc(PA, h), Tt_[:, h, :], TMc[:, 0, hs(h)], True, True, [Tt_, TMc], [PA[h // 8]])
            for q, pv in enumerate(v3(*PA)):
                k.cp("act" if q else "dve", Ah[:, q * 8:(q + 1) * 8, :], pv, [PA[q]], [Ah])
            for h in range(16):
                k.mm(pc(PU, h), Tt_[:, h, :], X0s[:, h, :], True, True, [Tt_, X0s], [PU[h // 8]])
            for q, pv in enumerate(v3(*PU)):
                k.cp("act" if q else "dve", U0s[:, q * 8:(q + 1) * 8, :], pv, [PU[q]], [U0s])
            yield
            PG = (PS[(pz + 0) % 8], PS[(pz + 1) % 8]); PQ = (PS[(pz + 2) % 8], PS[(pz + 3) % 8])
            for h in range(16):
                k.mm(pc(PG, h), Ah[:, h, :], TMc[:, 1, hs(h)], True, True, [Ah, TMc], [PG[h // 8]])
            for h in range(16):
                k.mm(pc(PQ, h), Ah[:, h, :], ARB_[:, h, :], True, True, [Ah, ARB_], [PQ[h // 8]])
            k.tt("dve", DWt[:], g.ident[0:64, 0:64].unsqueeze(1).to_broadcast([64, 16, 64]),
                 Wc[:, :, ch].unsqueeze(2).to_broadcast([64, 16, 64]), ALU.mult, [g.ident, Wc], [DWt])
            for q, pv in enumerate(v3(*PG)):
                k.tt("dve", GTs[:, q * 8:(q + 1) * 8, :], pv, DWt[:, q * 8:(q + 1) * 8, :], ALU.add, [PG[q], DWt], [GTs])
            for q, pv in enumerate(v3(*PQ)):
                k.tt("dve", QTs[:, q * 8:(q + 1) * 8, :], pv, RTc[:, q * 8:(q + 1) * 8, :], ALU.add, [PQ[q], RTc], [QTs])
            yield
            PY = (PS[(pz + 6) % 8], PS[(pz + 7) % 8]); PH = (PS[(pz + 4) % 8], PS[(pz + 5) % 8])
            for h in range(16):
                k.mm(pc(PY, h), ARB_[:, h, :], U0s[:, h, :], True, False, [ARB_, U0s], [PY[h // 8]])
                k.mm(pc(PY, h), ARK_[:, h, :], Vc[:, hs(h)], False, False, [ARK_, Vc], [PY[h // 8]])
                k.mm(pc(PY, h), QTs[:, h, :], Hb[:, h, :], False, True, [QTs, Hb], [PY[h // 8]])
            for h in range(16):
                k.mm(pc(PH, h), TMc[:, 1, hs(h)], U0s[:, h, :], True, False, [TMc, U0s], [PH[h // 8]])
                k.mm(pc(PH, h), TMc[:, 2, hs(h)], Vc[:, hs(h)], False, False, [TMc, Vc], [PH[h // 8]])
                k.mm(pc(PH, h), GTs[:, h, :], Hb[:, h, :], False, True, [GTs, Hb], [PH[h // 8]])
            yb = ybs[i]
            for q in range(2):
                k.cp("act", yb[:, q * 512:(q + 1) * 512], PY[q][0:64, :], [PY[q]], [yb])
            for q, pv in enumerate(v3(*PH)):
                k.cp("dve", Hb[:, q * 8:(q + 1) * 8, :], pv, [PH[q]], [Hb])
            k.dma(qa, d["Y"][z, b, ch * 64:(ch + 1) * 64, :], yb[:], reads=[yb], writes=[d["Y"]])
            yield

    from itertools import zip_longest
    for b in range(NBC):
        with Scope(k) as st:
            gens = [stream(0, b, st, "a"), stream(1, b, st, "b")]
            for _ in zip_longest(*gens):
                pass


def rw_readout(g):
    k = g.k
    PS = g.PS
    d = g.rw_scr
    with Scope(k) as st:
        onesf = k.sb("eonesf", [1, 128], F32, st)
        k.memset("dve", onesf[:], 1.0, [onesf])
        prow = k.sb("eprow", [1, D], F32, st)
        bcs = []
        for nm, row in (("lng", V_LNG), ("lnb", V_LNB)):
            t = k.sb("ebc_" + nm, [128, D], F32, st)
            k.dma("sp", prow[:], g.vecs[row:row + 1, :], reads=[g.vecs], writes=[prow])
            for half in range(2):
                k.mm(PS[half][:], onesf[:], prow[:, half * 512:(half + 1) * 512], True, True, [onesf, prow], [PS[half]])
                k.cp("act" if half else "dve", t[:, half * 512:(half + 1) * 512], PS[half][:], [PS[half]], [t])
            bcs.append(t)
        lng, lnb = bcs
        aT = k.sb("eaT", [128, 8, TT], BF16, st)
        ys = [k.sb(f"ey{i}", [128, D], F32, st) for i in range(2)]
        y1 = k.sb("ey1", [128, D], F32, st); bv = k.sb("ebv", [128, D], F32, st); gg = k.sb("egg", [128, D], F32, st)
        sq = k.sb("esq", [128, D], F32, st)
        st1 = k.sb("est1", [128, 16], F32, st); st2 = k.sb("est2", [128, 16], F32, st)
        h3 = lambda t: t[:].rearrange("p (h e) -> p h e", e=64)
        hb = lambda t: t[:].unsqueeze(2).to_broadcast([128, 16, 64])
        for b in range(NBC):
            for ti in range(18):
                t0 = ti * 128
                y = ys[ti % 2]
                k.dma("sp", y[:], d["Y"][0, b, t0:t0 + 128, :], reads=[d["Y"]], writes=[y])
                k.dma("act", y1[:], d["Y"][1, b, t0:t0 + 128, :], reads=[d["Y"]], writes=[y1])
                k.dma("sp", bv[:], d["BVG"][b, 0, t0:t0 + 128, :], reads=[d["BVG"]], writes=[bv])
                k.dma("act", gg[:], d["BVG"][b, 1, t0:t0 + 128, :], reads=[d["BVG"]], writes=[gg])
                k.tt("dve", y[:], y[:], y1[:], ALU.add, [y, y1], [y])
                k.op("dve", lambda e, o=st1[:], i_=h3(y): e.reduce_sum(o, i_, axis=AX.X), [y], [st1])
                k.ts("dve", st1[:], st1[:], 1.0 / 64, None, ALU.mult, None, [st1], [st1])
                k.tt("dve", h3(y), h3(y), hb(st1), ALU.subtract, [y, st1], [y])
                k.op("act", lambda e, o=sq[:], i_=y[:]: e.square(o, i_), [y], [sq])
                k.op("dve", lambda e, o=st2[:], i_=h3(sq): e.reduce_sum(o, i_, axis=AX.X), [sq], [st2])
                k.act(st2[:], st2[:], AF.Sqrt, [st2], [st2], bias=64e-5, scale=1.0 / 64)
                k.rcp(st2[:], st2[:], [st2], [st2])
                k.tt("dve", h3(y), h3(y), hb(st2), ALU.mult, [y, st2], [y])
                k.tt("dve", y[:], y[:], lng[:], ALU.mult, [y, lng], [y])
                k.tt("dve", y[:], y[:], lnb[:], ALU.add, [y, lnb], [y])
                k.tt("dve", y[:], y[:], bv[:], ALU.add, [y, bv], [y])
                k.tt("dve", y[:], y[:], gg[:], ALU.mult, [y, gg], [y])
                for half in range(2):
                    p = PS[2 + half]
                    for q in range(4):
                        c = half * 4 + q
                        k.tr(p[:, q * 128:(q + 1) * 128], y[:, c * 128:(c + 1) * 128], g.ident[:], [y, g.ident], [p])
                    k.cp("act" if half else "dve", aT[:, half * 4:(half + 1) * 4, t0:t0 + 128], p[:].rearrange("p (q t) -> p q t", q=4), [p], [(aT, half)])
            mixer_outproj(g, b, aT, g.rw_w_out[0], g.rw_w_out, False)


def _host_tables():
    import math
    p = np.arange(128)
    axis = (p % 64) // 32
    half = (p % 32) // 16
    pair = p % 16
    t = np.arange(S)
    pos = np.stack([t // 64, t % 64], 0).astype(np.float32)
    freq = (10000.0 ** (-np.arange(16, dtype=np.float32) / 16)).astype(np.float32)
    ang = pos[axis, :] * freq[pair][:, None]
    rope = np.stack([np.cos(ang), np.where(half[:, None] == 0, -np.sin(ang), np.sin(ang))], 0).astype(np.float32)
    j = np.arange(2048)
    perm = (j // 32) * 32 + ((j % 32) + 16) % 32
    return rope, perm


def _na_bias_table(rpb):
    qc = np.arange(64)[:, None]
    kc = np.arange(64)[None, :]
    cs = np.clip(qc - 8, 0, 48)
    col_in = (kc >= cs) & (kc < cs + 16)
    off = np.clip(kc - qc + 15, 0, 30)
    def rows(drs):
        out = np.full((64, 16, len(drs), 64), -1e9, np.float32)
        for i, dr in enumerate(drs):
            if dr is None or dr < -7 or dr > 7:
                continue
            vals = rpb[:, dr + 7, :][:, off]
            out[:, :, i, :] = np.where(col_in[:, None, :], np.transpose(vals, (1, 0, 2)), -1e9)
        return out.reshape(64, 16, len(drs) * 64)
    strips = [rows([None] + list(range(-4, 4)) + [None])]
    for r in (0, 1, 2, 3):
        strips.append(rows([row - r for row in range(0, 8)]))
    for r in (28, 29, 30, 31):
        strips.append(rows([row - r for row in range(24, 32)]))
    return np.ascontiguousarray(np.concatenate(strips, axis=2))


def _rw_consts():
    i = np.arange(128)
    s_ = (i % 64)[:, None]
    t_ = (i % 64)[None, :]
    strict_col = (i < 64)[None, :]
    m0 = np.where(strict_col, t_ > s_, t_ >= s_)
    m1 = np.where(strict_col, t_ < s_, t_ <= s_)
    same = (i // 64)[:, None] == (i // 64)[None, :]
    tri0 = same & (i[:, None] <= i[None, :])
    tri1 = same & (i[:, None] >= i[None, :])
    return np.stack([m0, m1, tri0, tri1, same], 0).astype(np.float32)


def make_in_maps(inp):
    f = lambda a: np.ascontiguousarray(np.asarray(a, dtype=np.float32))
    rope, perm = _host_tables()
    vecs = np.concatenate([
        f(inp["norm_g"]).reshape(24, D), f(inp["ada_b"]).reshape(36, D),
        f(inp["rw_mu"])[0], f(inp["rw_w0"])[0], f(inp["rw_a0"])[0],
        f(inp["rw_k_k"]), f(inp["rw_k_a"]), f(inp["rw_ln_g"]), f(inp["rw_ln_b"]),
        f(inp["rw_r_k"]).reshape(1, D)], axis=0)
    assert vecs.shape == (NV, D)
    da_w_in = f(inp["da_w_in"])
    shared = {
        "vecs": np.ascontiguousarray(vecs), "ident": np.eye(128, dtype=np.float32),
        "ada_w": f(inp["ada_w"]), "ffn_w_in": f(inp["ffn_w_in"]), "ffn_w_out": f(inp["ffn_w_out"]),
        "da_w_in": da_w_in, "da_w_sw": np.ascontiguousarray(da_w_in[:, :, :2048][:, :, perm]),
        "da_w_out": f(inp["da_w_out"]), "da_lambda": f(inp["da_lambda"]), "da_subln_g": f(inp["da_subln_g"]),
        "rope": rope,
        "na_w_in": f(inp["na_w_in"]), "na_w_out": f(inp["na_w_out"]), "na_bias": _na_bias_table(f(inp["na_rpb"])[0]),
        "rw_w_in": f(inp["rw_w_in"]), "rw_w_out": f(inp["rw_w_out"]),
        "rw_w1": f(inp["rw_w1"]), "rw_w2": f(inp["rw_w2"]), "rw_a1": f(inp["rw_a1"]), "rw_a2": f(inp["rw_a2"]),
        "rw_g1": f(inp["rw_g1"]), "rw_g2": f(inp["rw_g2"]), "rwconst": _rw_consts(),
    }
    x = f(inp["x"]); ctx = f(inp["ctx"]); c = f(inp["c"]); cc = f(inp["c_ctx"])
    maps = []
    for i in range(8):
        m = dict(shared)
        m["x"] = np.ascontiguousarray(x[2 * i:2 * i + 2]); m["ctx"] = np.ascontiguousarray(ctx[2 * i:2 * i + 2])
        m["cvec"] = np.ascontiguousarray(np.concatenate([c[2 * i:2 * i + 2], cc[None, :]], 0))
        maps.append(m)
    return maps


def kernel(**inputs):
    nc = build_program()
    maps = make_in_maps(inputs)
    res = run_bass_kernel_spmd(nc, maps, core_ids=list(range(8)))
    return np.concatenate([np.asarray(r["out"], dtype=np.float32) for r in res.results], axis=0)
```
